# Optimizing a Trainium2 kernel written in Bass

```python
import math
import jax, jax.numpy as jnp
from jax import lax
import numpy as np

D_MODEL = 1024
BATCH = 8
SEQ = 2048
DEPTH = 2

N_MEM = 256
HEAD_DIM = 128
DN_HEADS = D_MODEL // HEAD_DIM
FOX_HEADS = D_MODEL // HEAD_DIM
MEM_HEADS = 4
MIX_WIDTH = D_MODEL
MEM_WIDTH = MEM_HEADS * HEAD_DIM
D_FF = 4 * D_MODEL
CONV_WIDTH = 4
CHUNK = 64
Q_BLOCK = 128
EPS = 1e-6
N_DN = (DEPTH + 1) // 2
N_FOX = DEPTH // 2
DN_IN = 4 * MIX_WIDTH + 2 * DN_HEADS + MEM_WIDTH
FOX_IN = 4 * MIX_WIDTH + FOX_HEADS + MEM_WIDTH
OUT_IN = MIX_WIDTH + MEM_WIDTH

kernel_name = "hybrid_deltanet_fox_memory_decoder"


def rms_norm(x, w):
    xf = x.astype(jnp.float32)
    y = xf * lax.rsqrt(jnp.mean(xf * xf, axis=-1, keepdims=True) + EPS)
    return (y * w.astype(jnp.float32)).astype(x.dtype)


def l2_norm(x):
    xf = x.astype(jnp.float32)
    return xf * lax.rsqrt(jnp.sum(xf * xf, axis=-1, keepdims=True) + EPS)


def split_heads(t, heads):
    return t.reshape(t.shape[:-1] + (heads, HEAD_DIM))


def causal_depthwise_conv(x, w):
    c = x.shape[-1]
    return lax.conv_general_dilated(
        x, w[:, None, :].astype(x.dtype), window_strides=(1,),
        padding=[(CONV_WIDTH - 1, 0)], dimension_numbers=("NWC", "WIO", "NWC"),
        feature_group_count=c)


def chunked_gated_delta_rule(q, k, v, g, beta):
    bsz, seq, heads, _ = q.shape
    n = seq // CHUNK

    def chunks(t):
        return t.reshape(bsz, n, CHUNK, heads, -1).transpose(1, 0, 3, 2, 4)

    qc, kc, vc = chunks(q), chunks(k), chunks(v)
    gc = chunks(g[..., None])[..., 0]
    bc = chunks(beta[..., None])[..., 0]
    gcum = jnp.cumsum(gc, axis=-1)
    causal = jnp.tril(jnp.ones((CHUNK, CHUNK), dtype=bool))
    strict = jnp.tril(jnp.ones((CHUNK, CHUNK), dtype=bool), -1)
    diff = gcum[..., :, None] - gcum[..., None, :]
    decay = jnp.where(causal, jnp.exp(jnp.where(causal, diff, 0.0)), 0.0)
    kb = kc * bc[..., None]
    a_mat = jnp.where(strict, jnp.einsum("nbhik,nbhjk->nbhij", kb, kc) * decay, 0.0)
    eye = jnp.eye(CHUNK, dtype=jnp.float32)
    t_mat = lax.linalg.triangular_solve(
        eye + a_mat, jnp.broadcast_to(eye, a_mat.shape), left_side=True, lower=True)
    u = jnp.einsum("nbhij,nbhjv->nbhiv", t_mat, vc * bc[..., None])
    w = jnp.einsum("nbhij,nbhjk->nbhik", t_mat, kb * jnp.exp(gcum)[..., None])
    qk = jnp.where(causal, jnp.einsum("nbhik,nbhjk->nbhij", qc, kc) * decay, 0.0)

    def step(state, inp):
        q_i, k_i, u_i, w_i, qk_i, g_i = inp
        v_new = u_i - jnp.einsum("bhck,bhkv->bhcv", w_i, state)
        out = (jnp.einsum("bhck,bhkv->bhcv", q_i * jnp.exp(g_i)[..., None], state)
               + jnp.einsum("bhij,bhjv->bhiv", qk_i, v_new))
        g_last = g_i[..., -1:]
        k_dec = k_i * jnp.exp(g_last - g_i)[..., None]
        state = state * jnp.exp(g_last)[..., None] + jnp.einsum("bhck,bhcv->bhkv", k_dec, v_new)
        return state, out

    state0 = jnp.zeros((bsz, heads, q.shape[-1], v.shape[-1]), jnp.float32)
    _, o = lax.scan(step, state0, (qc, kc, u, w, qk, gcum))
    return o.transpose(1, 0, 3, 2, 4).reshape(bsz, seq, heads, -1)


def gated_deltanet(h, w_in, conv_w, a_log, dt_bias, o_norm_w):
    bsz, seq, _ = h.shape
    proj = h @ w_in
    qkv = proj[..., : 3 * MIX_WIDTH]
    z = proj[..., 3 * MIX_WIDTH: 4 * MIX_WIDTH]
    a = proj[..., 4 * MIX_WIDTH: 4 * MIX_WIDTH + DN_HEADS]
    b = proj[..., 4 * MIX_WIDTH + DN_HEADS: 4 * MIX_WIDTH + 2 * DN_HEADS]
    q_mem = proj[..., 4 * MIX_WIDTH + 2 * DN_HEADS:]
    qkv = jax.nn.silu(causal_depthwise_conv(qkv, conv_w))
    q = l2_norm(split_heads(qkv[..., :MIX_WIDTH], DN_HEADS)) * (HEAD_DIM ** -0.5)
    k = l2_norm(split_heads(qkv[..., MIX_WIDTH: 2 * MIX_WIDTH], DN_HEADS))
    v = split_heads(qkv[..., 2 * MIX_WIDTH:], DN_HEADS).astype(jnp.float32)
    beta = jax.nn.sigmoid(b.astype(jnp.float32))
    g = -jnp.exp(a_log.astype(jnp.float32)) * jax.nn.softplus(
        a.astype(jnp.float32) + dt_bias.astype(jnp.float32))
    o = chunked_gated_delta_rule(q, k, v, g, beta)
    o = rms_norm(o, o_norm_w) * jax.nn.silu(split_heads(z, DN_HEADS).astype(jnp.float32))
    return o.reshape(bsz, seq, MIX_WIDTH).astype(h.dtype), q_mem


def forgetting_attention(h, w_in, f_bias, q_norm_w, k_norm_w):
    bsz, seq, _ = h.shape
    proj = h @ w_in
    q = split_heads(proj[..., :MIX_WIDTH], FOX_HEADS)
    k = split_heads(proj[..., MIX_WIDTH: 2 * MIX_WIDTH], FOX_HEADS)
    v = split_heads(proj[..., 2 * MIX_WIDTH: 3 * MIX_WIDTH], FOX_HEADS)
    gate = proj[..., 3 * MIX_WIDTH: 4 * MIX_WIDTH]
    f_logit = proj[..., 4 * MIX_WIDTH: 4 * MIX_WIDTH + FOX_HEADS]
    q_mem = proj[..., 4 * MIX_WIDTH + FOX_HEADS:]
    q = rms_norm(q, q_norm_w).astype(jnp.float32) * (HEAD_DIM ** -0.5)
    k = rms_norm(k, k_norm_w).astype(jnp.float32)
    log_f = jax.nn.log_sigmoid(f_logit.astype(jnp.float32) + f_bias.astype(jnp.float32))
    f_cum = jnp.cumsum(log_f, axis=1).transpose(0, 2, 1)
    nb = seq // Q_BLOCK
    q_blocks = q.reshape(bsz, nb, Q_BLOCK, FOX_HEADS, HEAD_DIM).transpose(1, 0, 2, 3, 4)
    f_blocks = f_cum.reshape(bsz, FOX_HEADS, nb, Q_BLOCK).transpose(2, 0, 1, 3)
    k_pos = jnp.arange(seq)

    def block(args):
        qb, fb, i = args
        s = jnp.einsum("bqhd,bkhd->bhqk", qb, k)
        bias = fb[..., :, None] - f_cum[:, :, None, :]
        q_pos = i * Q_BLOCK + jnp.arange(Q_BLOCK)
        mask = k_pos[None, :] <= q_pos[:, None]
        p = jax.nn.softmax(jnp.where(mask, s + bias, -jnp.inf), axis=-1)
        return jnp.einsum("bhqk,bkhd->bqhd", p.astype(v.dtype), v)

    o = lax.map(block, (q_blocks, f_blocks, jnp.arange(nb)))
    o = o.transpose(1, 0, 2, 3, 4).reshape(bsz, seq, MIX_WIDTH)
    o = o.astype(jnp.float32) * jax.nn.sigmoid(gate.astype(jnp.float32))
    return o.astype(h.dtype), q_mem


def memory_attention(q_mem, mem_k, mem_v, q_norm_w):
    bsz, seq, _ = q_mem.shape
    q = rms_norm(split_heads(q_mem, MEM_HEADS), q_norm_w).astype(jnp.float32) * (HEAD_DIM ** -0.5)
    p = jax.nn.softmax(jnp.einsum("bthd,bmhd->bhtm", q, mem_k), axis=-1)
    o = jnp.einsum("bhtm,bmhd->bthd", p.astype(mem_v.dtype), mem_v)
    return o.reshape(bsz, seq, MEM_WIDTH).astype(q_mem.dtype)


def setup_inputs(seed: int = 0) -> dict:
    key = jax.random.key(seed)
    ks = jax.random.split(key, 20)
    f32 = jnp.float32

    def dense(k, shape, fan_in):
        return jax.random.normal(k, shape, f32) * fan_in ** -0.5

    def gain(k, shape):
        return 1.0 + 0.02 * jax.random.normal(k, shape, f32)

    dt = jnp.exp(jax.random.uniform(ks[9], (N_DN, DN_HEADS), f32, math.log(1e-3), math.log(1e-1)))
    return {
        "x": jax.random.normal(ks[0], (BATCH, SEQ, D_MODEL), f32),
        "mem": jax.random.normal(ks[1], (BATCH, N_MEM, D_MODEL), f32),
        "mem_norm_w": gain(ks[2], (D_MODEL,)),
        "w_mem_kv": dense(ks[3], (D_MODEL, 2 * MEM_WIDTH), D_MODEL),
        "mem_k_norm_w": gain(ks[4], (HEAD_DIM,)),
        "norm1_w": gain(ks[5], (DEPTH, D_MODEL)),
        "dn_w_in": dense(ks[6], (N_DN, D_MODEL, DN_IN), D_MODEL),
        "dn_conv_w": dense(ks[7], (N_DN, CONV_WIDTH, 3 * MIX_WIDTH), CONV_WIDTH),
        "dn_a_log": jnp.log(jax.random.uniform(ks[8], (N_DN, DN_HEADS), f32, 1.0, 16.0)),
        "dn_dt_bias": dt + jnp.log(-jnp.expm1(-dt)),
        "dn_o_norm_w": gain(ks[10], (N_DN, HEAD_DIM)),
        "fox_w_in": dense(ks[11], (N_FOX, D_MODEL, FOX_IN), D_MODEL),
        "fox_f_bias": jax.random.uniform(ks[12], (N_FOX, FOX_HEADS), f32, 1.0, 4.0),
        "fox_q_norm_w": gain(ks[13], (N_FOX, HEAD_DIM)),
        "fox_k_norm_w": gain(ks[14], (N_FOX, HEAD_DIM)),
        "memq_norm_w": gain(ks[15], (DEPTH, HEAD_DIM)),
        "w_out": dense(ks[16], (DEPTH, OUT_IN, D_MODEL), OUT_IN),
        "norm2_w": gain(ks[17], (DEPTH, D_MODEL)),
        "w_mlp1": dense(ks[18], (DEPTH, D_MODEL, D_FF), D_MODEL),
        "w_mlp2": dense(ks[19], (DEPTH, D_FF, D_MODEL), D_FF),
    }


def reference(x, mem, mem_norm_w, w_mem_kv, mem_k_norm_w, norm1_w, dn_w_in, dn_conv_w,
              dn_a_log, dn_dt_bias, dn_o_norm_w, fox_w_in, fox_f_bias, fox_q_norm_w,
              fox_k_norm_w, memq_norm_w, w_out, norm2_w, w_mlp1, w_mlp2):
    mem_kv = rms_norm(mem, mem_norm_w) @ w_mem_kv
    mem_k = rms_norm(split_heads(mem_kv[..., :MEM_WIDTH], MEM_HEADS), mem_k_norm_w).astype(jnp.float32)
    mem_v = split_heads(mem_kv[..., MEM_WIDTH:], MEM_HEADS)
    for i in range(DEPTH):
        h = rms_norm(x, norm1_w[i])
        j = i // 2
        if i % 2 == 0:
            mix, q_mem = gated_deltanet(h, dn_w_in[j], dn_conv_w[j], dn_a_log[j],
                                        dn_dt_bias[j], dn_o_norm_w[j])
        else:
            mix, q_mem = forgetting_attention(h, fox_w_in[j], fox_f_bias[j],
                                              fox_q_norm_w[j], fox_k_norm_w[j])
        mem_out = memory_attention(q_mem, mem_k, mem_v, memq_norm_w[i])
        x = x + jnp.concatenate([mix, mem_out], axis=-1) @ w_out[i]
        h = rms_norm(x, norm2_w[i])
        x = x + jnp.square(jax.nn.relu(h @ w_mlp1[i])) @ w_mlp2[i]
    return x
```

```python
import numpy as np
import ml_dtypes
import concourse.bass as bass
import concourse.mybir as mybir
from concourse.bass_utils import run_bass_kernel_spmd
from contextlib import ExitStack

F32 = mybir.dt.float32
F32R = mybir.dt.float32r
BF16 = mybir.dt.bfloat16
AF = mybir.ActivationFunctionType
ALU = mybir.AluOpType
AX = mybir.AxisListType

T = 2048
D = 1024
KC = 8
NMEM = 256
HD = 128
DN_IN = 4624
FOX_IN = 4616
DFF = 4096
EPS = 1e-6
NEG = -30000.0
N_CORES = 8

C_N1 = 0
C_N2 = 16
C_MEMN = 32
C_MEMK = 40
C_MEMQ = 41
C_FOXQ = 43
C_FOXK = 44
C_DNO = 45
C_CONV = 46
C_HS = 142
NCOLS = 148

K_IDENT = 0
K_CAUSAL = 128
K_ONES = 256
K_SELG = 384
K_BD6 = 388
K_ON6 = 644
NBF = 772
K_MLOW = 772
K_MUPS = 900
K_MUPI = 1028
K_SEL8 = 1156
K_E6 = 2180
NCONST = 2184


def _dt_bytes(dt):
    s = str(dt)
    if '32' in s:
        return 4
    if '16' in s:
        return 2
    if '64' in s:
        return 8
    if '8' in s:
        return 1
    raise ValueError(s)


class Sched:
    SEM_LIMIT = 30000

    def __init__(self, nc, es):
        self.nc = nc
        self.es = es
        self.eng = {'pe': nc.tensor, 'dve': nc.vector, 'act': nc.scalar, 'pool': nc.gpsimd, 'sp': nc.sync}
        self.esem = {}
        self.ecnt = {}
        self.eepoch = {}
        for e in ['pe', 'dve', 'act', 'pool']:
            self._new_epoch(e)
        self.waited = {e: {} for e in self.eng}
        self.regions = {}
        self.dq = {}
        for q, n in (('sp', 16), ('pool', 16), ('act', 6)):
            self.dq[q] = dict(sems=[es.enter_context(nc.semaphore(f"dma_{q}{i}")) for i in range(n)],
                              cnt=[0] * n, nxt=0)
        self.out_events = []
        self.n_inst = {e: 0 for e in self.eng}
        self.n_wait = {e: 0 for e in self.eng}

    def _new_epoch(self, e):
        ep = self.eepoch.get(e, -1) + 1
        self.eepoch[e] = ep
        self.esem[e] = self.es.enter_context(self.nc.semaphore(f"e_{e}_{ep}"))
        self.ecnt[e] = 0

    @staticmethod
    def region(ap):
        a = ap.ap
        pstride, pcount = a[0]
        off = ap.offset
        if pstride > 0:
            p0 = off // pstride
            f0 = off % pstride
        else:
            p0 = 0
            f0 = off
        ext = 1
        for st, cnt in a[1:]:
            ext += (cnt - 1) * abs(st)
        b = _dt_bytes(ap.dtype)
        return (ap.tensor.name, p0, p0 + pcount, f0 * b, (f0 + ext) * b)

    def _regs(self, aps):
        out = []
        for a in aps:
            if a is None:
                continue
            if isinstance(a, tuple):
                out.append(a)
            else:
                out.append(self.region(a))
        return out

    def _deps(self, reads, writes):
        deps = []
        for name, p0, p1, f0, f1 in reads:
            for ent in self.regions.get(name, ()):
                if ent[4] == 'w' and ent[0] < p1 and p0 < ent[1] and ent[2] < f1 and f0 < ent[3]:
                    deps.append(ent[5])
        for name, p0, p1, f0, f1 in writes:
            for ent in self.regions.get(name, ()):
                if ent[0] < p1 and p0 < ent[1] and ent[2] < f1 and f0 < ent[3]:
                    deps.append(ent[5])
        return deps

    def _record(self, reads, writes, ev):
        for name, p0, p1, f0, f1 in writes:
            lst = self.regions.setdefault(name, [])
            lst[:] = [e for e in lst if not (p0 <= e[0] and e[1] <= p1 and f0 <= e[2] and e[3] <= f1)]
            lst.append((p0, p1, f0, f1, 'w', ev))
        for name, p0, p1, f0, f1 in reads:
            lst = self.regions.setdefault(name, [])
            for i, e in enumerate(lst):
                if e[4] == 'r' and e[0] == p0 and e[1] == p1 and e[2] == f0 and e[3] == f1 and e[5][0] is ev[0]:
                    lst[i] = (p0, p1, f0, f1, 'r', ev)
                    break
            else:
                lst.append((p0, p1, f0, f1, 'r', ev))

    def _wait(self, e, deps, skip_sem=None):
        need = {}
        for sem, val in deps:
            if sem is skip_sem:
                continue
            k = id(sem)
            if self.waited[e].get(k, 0) >= val:
                continue
            if k not in need or need[k][1] < val:
                need[k] = (sem, val)
        for k, (sem, val) in need.items():
            self.eng[e].wait_ge(sem, val)
            self.waited[e][k] = val
            self.n_wait[e] += 1

    def op(self, e, fn, reads=(), writes=(), inc=True):
        reads = self._regs(reads)
        writes = self._regs(writes)
        if e == 'pe':
            writes = [(nm, (p0 // 32) * 32, -(-p1 // 32) * 32, 0, 2048) for nm, p0, p1, f0, f1 in writes]
        deps = self._deps(reads, writes)
        if e != 'pe':
            own = self.esem[e]
            for nm, p0, p1, f0, f1 in list(reads) + list(writes):
                if nm.startswith('ps'):
                    for ent in self.regions.get(nm, ()):
                        if ent[5][0] is not own:
                            deps.append(ent[5])
        self._wait(e, deps, skip_sem=(self.esem['pe'] if e == 'pe' else None))
        inst = fn()
        self.n_inst[e] += 1
        if inc:
            self.ecnt[e] += 1
            inst.then_inc(self.esem[e], 1)
            ev = (self.esem[e], self.ecnt[e])
        else:
            assert e == 'pe'
            ev = (self.esem[e], self.ecnt[e] + 1)
        self._record(reads, writes, ev)
        if inc and self.ecnt[e] >= self.SEM_LIMIT:
            self._new_epoch(e)
        return ev

    def dma(self, q, out, in_, reads=(), writes=(), is_output=False, **kw):
        rr = list(reads)
        ww = list(writes)
        if str(in_.space) != 'DRAM':
            rr.append(in_)
        if str(out.space) != 'DRAM':
            ww.append(out)
        rr = self._regs(rr)
        ww = self._regs(ww)
        deps = self._deps(rr, ww)
        dq = self.dq[q]
        i = dq['nxt']
        dq['nxt'] = (i + 1) % len(dq['sems'])
        sem = dq['sems'][i]
        if dq['cnt'][i] > 0:
            deps.append((sem, dq['cnt'][i]))
        self._wait(q, deps)
        dq['cnt'][i] += 16
        self.eng[q].dma_start(out=out, in_=in_, **kw).then_inc(sem, 16)
        self.n_inst[q] += 1
        ev = (sem, dq['cnt'][i])
        self._record(rr, ww, ev)
        if is_output:
            self.out_events.append(ev)
        return ev

    def finish(self):
        self._wait('sp', self.out_events)


def dreg(name, t0, t1):
    return (name, 0, 1, t0, t1)


def build_program(layers=(0, 1), dbg=None):
    nc = bass.Bass("TRN2", target_bir_lowering=False)

    def dram(name, shape, dt=F32, kind="ExternalInput"):
        return nc.dram_tensor(name, shape, dt, kind=kind).ap()

    xT_d = dram("xT", [D, T])
    memT_d = dram("memT", [D, NMEM])
    cols_d = dram("cols", [128, NCOLS])
    consts_d = dram("consts", [128, NCONST])
    wmemkv_d = dram("w_mem_kv", [D, 1024])
    dnw_d = dram("dn_w_in", [D, DN_IN])
    foxw_d = dram("fox_w_in", [D, FOX_IN])
    wout_d = dram("w_out", [2, 1536, D])
    w1_d = dram("w_mlp1", [2, D, DFF])
    w2_d = dram("w_mlp2", [2, DFF, D])
    outT_d = dram("outT", [D, T], kind="ExternalOutput")
    xs1_d = dram("xs1", [D, T], kind="Internal")
    xs2_d = dram("xs2", [D, T], kind="Internal")
    fsc_d = dram("fsc", [6, 8, T], BF16, kind="Internal")
    rows_d = dram("rowsd", [3, 8, T], F32, kind="Internal")
    dbg_d = None
    if dbg is not None:
        dbg_d = dram("dbg", [12 * 128, T], BF16, kind="ExternalOutput")

    es = ExitStack()
    with es:
        S = Sched(nc, es)
        ARENA_BYTES = 178 * 1024
        arena = es.enter_context(nc.sbuf_tensor("arena", [128, ARENA_BYTES // 4], F32))
        cols = es.enter_context(nc.sbuf_tensor("colsb", [128, NCOLS], F32))
        cst = es.enter_context(nc.sbuf_tensor("cst", [128, NCONST], F32))
        cbf = es.enter_context(nc.sbuf_tensor("cbf", [128, NBF], BF16))
        nmn = es.enter_context(nc.sbuf_tensor("nmn", [128, 3072], F32R))
        memKT = es.enter_context(nc.sbuf_tensor("memKT", [128, 4, NMEM], BF16))
        memV = es.enter_context(nc.sbuf_tensor("memV", [128, 2, 512], BF16))
        psb = [es.enter_context(nc.psum_tensor(f"ps{i}", [128, 512], F32)) for i in range(8)]

        def view(off, shape, dt=F32):
            n = 1
            for s in shape[1:]:
                n *= s
            nb = n * _dt_bytes(dt)
            assert off % 4 == 0 and nb % 4 == 0 and off + nb <= ARENA_BYTES, (off, nb)
            ap = arena[0:128, off // 4:(off + nb) // 4]
            if dt != F32:
                ap = ap.bitcast(dt)
            if len(shape) == 3:
                ap = ap.rearrange("p (a b) -> p a b", a=shape[1])
            elif len(shape) == 4:
                ap = ap.rearrange("p (a b c) -> p a b c", a=shape[1], b=shape[2])
            if shape[0] != 128:
                ap = ap[0:shape[0]]
            return ap

        ps_rr = [0]
        ps_lim = [8]

        def bank(i=None):
            if i is None:
                i = ps_rr[0] % ps_lim[0]
                ps_rr[0] = (i + 1) % ps_lim[0]
            return psb[i]

        def is_ap(x):
            return not isinstance(x, (int, float)) and x is not None

        def mm(out, lhsT, rhs, start=True, stop=True):
            S.op('pe', lambda: nc.tensor.matmul(out, lhsT=lhsT, rhs=rhs, start=start, stop=stop),
                 reads=[lhsT, rhs], writes=[out], inc=stop)

        def transpose(out, in_, ident):
            S.op('pe', lambda: nc.tensor.transpose(out, in_, ident), reads=[in_, ident], writes=[out])

        def act(out, in_, func, scale=1.0, bias=None, accum=None):
            reads = [in_]
            kw = dict(out=out, in_=in_, func=func, scale=scale)
            if is_ap(scale):
                reads.append(scale)
            if bias is not None:
                kw['bias'] = bias
                if is_ap(bias):
                    reads.append(bias)
            writes = [out]
            if accum is not None:
                kw['accum_out'] = accum
                writes.append(accum)
            S.op('act', lambda: nc.scalar.activation(**kw), reads=reads, writes=writes)

        def ts(e, out, in0, s1, op0, s2=None, op1=None):
            reads = [in0] + [s for s in (s1, s2) if is_ap(s)]
            eng = nc.vector if e == 'dve' else nc.gpsimd
            kw = dict(out=out, in0=in0, scalar1=s1, scalar2=s2, op0=op0)
            if op1 is not None:
                kw['op1'] = op1
            S.op(e, lambda: eng.tensor_scalar(**kw), reads=reads, writes=[out])

        def tt(e, out, in0, in1, op):
            eng = nc.vector if e == 'dve' else nc.gpsimd
            S.op(e, lambda: eng.tensor_tensor(out=out, in0=in0, in1=in1, op=op), reads=[in0, in1], writes=[out])

        def stt(out, in0, scalar, in1, op0, op1):
            reads = [in0, in1] + ([scalar] if is_ap(scalar) else [])
            S.op('dve', lambda: nc.vector.scalar_tensor_tensor(out=out, in0=in0, scalar=scalar, in1=in1, op0=op0, op1=op1),
                 reads=reads, writes=[out])

        def copy(e, out, in_):
            if e == 'act':
                S.op('act', lambda: nc.scalar.copy(out=out, in_=in_), reads=[in_], writes=[out])
            else:
                eng = nc.vector if e == 'dve' else nc.gpsimd
                S.op(e, lambda: eng.tensor_copy(out=out, in_=in_), reads=[in_], writes=[out])

        def memset(e, out, val):
            eng = nc.vector if e == 'dve' else nc.gpsimd
            S.op(e, lambda: eng.memset(out, val), writes=[out])

        def recip(out, in_):
            S.op('dve', lambda: nc.vector.reciprocal(out=out, in_=in_), reads=[in_], writes=[out])

        def wload(dst, src_rows_cols):
            S.dma('pool', dst, src_rows_cols.rearrange("(kc p) n -> p kc n", p=128))

        S.dma('sp', cols[:], cols_d)
        S.dma('sp', cst[:], consts_d)
        copy('dve', cbf[:], cst[:, 0:NBF])
        ident_bf = cbf[:, K_IDENT:K_IDENT + 128]
        causal_bf = cbf[:, K_CAUSAL:K_CAUSAL + 128]
        ones_bf = cbf[:, K_ONES:K_ONES + 128]

        def fm_rmsnorm_gen(src, wcol, dst, kcn, tn, dn, tmp_off, post_ln_bias=0.0, pbank=None):
            sq = view(tmp_off, [128, kcn, 512], BF16)
            rs = view(tmp_off + kcn * 1024, [128, 512], F32)
            for t0 in range(0, tn, 512):
                tw = min(512, tn - t0)
                pb = bank() if pbank is None else (pbank[(t0 // 512) % len(pbank)] if isinstance(pbank, list) else pbank)
                for kc in range(kcn):
                    act(sq[:, kc, :tw], src[:, kc, t0:t0 + tw], AF.Square)
                for kc in range(kcn):
                    mm(pb[:, :tw], ones_bf, sq[:, kc, :tw], start=(kc == 0), stop=(kc == kcn - 1))
                yield
                act(rs[:, :tw], pb[:, :tw], AF.Ln, scale=1.0 / dn, bias=EPS)
                act(rs[:, :tw], rs[:, :tw], AF.Exp, scale=-0.5, bias=post_ln_bias)
                for kc in range(kcn):
                    stt(dst[:, kc, t0:t0 + tw], src[:, kc, t0:t0 + tw], wcol[:, kc:kc + 1], rs[:, :tw],
                        ALU.mult, ALU.mult)
                yield

        def fm_rmsnorm(*a, **k):
            for _ in fm_rmsnorm_gen(*a, **k):
                pass

        OFF_HT = 0
        OFF_MIX = 32 * 1024
        OFF_WORK = 80 * 1024
        hT = view(OFF_HT, [128, KC, T], BF16)
        mixT = view(OFF_MIX, [128, 12, T], BF16)

        def phase_mem():
            o = OFF_WORK
            mT = view(o, [128, KC, NMEM], F32); o += KC * NMEM * 4
            mn = view(o, [128, KC, NMEM], BF16); o += KC * NMEM * 2
            wkv = view(o, [128, KC, 1024], BF16); o += KC * 1024 * 2
            kf = view(o, [128, 1, NMEM], F32); o += NMEM * 4
            tmp = o
            S.dma('sp', mT, memT_d.rearrange("(kc p) n -> p kc n", p=128))
            wload(wkv[:, :, 0:512], wmemkv_d[:, 0:512])
            wload(wkv[:, :, 512:1024], wmemkv_d[:, 512:1024])
            fm_rmsnorm(mT, cols[:, C_MEMN:C_MEMN + 8], mn, KC, NMEM, D, tmp)
            for h in range(4):
                pb = bank()
                for kc in range(KC):
                    mm(pb[:, :NMEM], wkv[:, kc, h * 128:(h + 1) * 128], mn[:, kc, :], start=(kc == 0), stop=(kc == KC - 1))
                copy('dve', kf[:, 0, :], pb[:, :NMEM])
                fm_rmsnorm(kf, cols[:, C_MEMK:C_MEMK + 1], memKT[:, h:h + 1, :], 1, NMEM, HD, tmp)
            for mt in range(2):
                pb = bank()
                for kc in range(KC):
                    mm(pb[:, :], mn[:, kc, mt * 128:(mt + 1) * 128], wkv[:, kc, 512:1024], start=(kc == 0), stop=(kc == KC - 1))
                copy('act', memV[:, mt, :], pb[:, :])

        def phase_norm1(layer, x_src, x_src_name):
            o = OFF_WORK
            xb = [view(o + i * 16384, [128, KC, 512], F32) for i in range(2)]
            tmp = o + 32768
            for tq in range(4):
                xt = xb[tq % 2]
                rd = [dreg(x_src_name, tq * 512, (tq + 1) * 512)] if x_src_name else []
                S.dma('sp', xt, x_src[:, tq * 512:(tq + 1) * 512].rearrange("(kc p) n -> p kc n", p=128), reads=rd)
                fm_rmsnorm(xt, cols[:, C_N1 + layer * 8:C_N1 + layer * 8 + 8], hT[:, :, tq * 512:(tq + 1) * 512],
                           KC, 512, D, tmp)

        def phase_memattn(layer, w_in_d, qm_off, work_off):
            o = work_off
            wq = [view(o + i * 2048, [128, KC, 128], BF16) for i in range(2)]; o += 4096
            qf = view(o, [128, 1, T], F32); o += T * 4
            qn = [view(o + i * T * 2, [128, 1, T], BF16) for i in range(2)]; o += T * 4
            pT = [view(o + i * 1024, [128, 512], BF16) for i in range(4)]; o += 4096
            rc = view(o, [128, 512], F32); o += 2048
            tmp = o
            pro_bank = psb[7]

            def prologue(h):
                c0 = qm_off + h * 128
                wb = wq[h % 2]
                wload(wb, w_in_d[:, c0:c0 + 128])
                for tq in range(4):
                    for kc in range(KC):
                        mm(pro_bank[:, :], wb[:, kc, :], hT[:, kc, tq * 512:(tq + 1) * 512], start=(kc == 0), stop=(kc == KC - 1))
                    yield
                    copy('act', qf[:, 0, tq * 512:(tq + 1) * 512], pro_bank[:, :])
                    yield
                yield from fm_rmsnorm_gen(qf, cols[:, C_MEMQ + layer:C_MEMQ + layer + 1], qn[h % 2], 1, T, HD, tmp,
                                          pbank=pro_bank)

            def attention(h):
                qh = qn[h % 2]

                def stA(i):
                    tq, mt = divmod(i, 2)
                    mm(psb[4 + i % 3][:, :], memKT[:, h, mt * 128:(mt + 1) * 128], qh[:, 0, tq * 512:(tq + 1) * 512])

                def stB(i):
                    act(pT[i % 4], psb[4 + i % 3][:, :], AF.Exp, scale=float(HD) ** -0.5)

                def stC(i):
                    tq, mt = divmod(i, 2)
                    par = (h * 4 + tq) % 2
                    po, pl = psb[par], psb[2 + par]
                    p = pT[i % 4]
                    mm(po[:, :], memV[:, mt, h * 128:(h + 1) * 128], p, start=(mt == 0), stop=(mt == 1))
                    mm(pl[:, :], ones_bf, p, start=(mt == 0), stop=(mt == 1))
                    if mt == 1:
                        recip(rc, pl[:, :])
                        tt('dve', mixT[:, 8 + h, tq * 512:(tq + 1) * 512], po[:, :], rc, ALU.mult)

                for i in range(8 + 2):
                    if i < 8:
                        stA(i)
                    if 0 <= i - 2 < 8:
                        stC(i - 2)
                    if 0 <= i - 1 < 8:
                        stB(i - 1)
                    yield

            for _ in prologue(0):
                pass
            for h in range(4):
                gens = [attention(h)]
                if h + 1 < 4:
                    gens.append(prologue(h + 1))
                while gens:
                    for g_ in list(gens):
                        try:
                            next(g_)
                        except StopIteration:
                            gens.remove(g_)

        def phase_outproj(layer, x_src, x_src_name):
            wo = view(OFF_HT, [128, 12, 1024], BF16)
            o = OFF_WORK
            xb = [view(o + i * 16384, [128, KC, 512], F32) for i in range(2)]
            wload(wo[:, :, 0:512], wout_d[layer, :, 0:512])
            wload(wo[:, :, 512:1024], wout_d[layer, :, 512:1024])
            for tq in range(4):
                xt = xb[tq % 2]
                rd = [dreg(x_src_name, tq * 512, (tq + 1) * 512)] if x_src_name else []
                S.dma('sp', xt, x_src[:, tq * 512:(tq + 1) * 512].rearrange("(kc p) n -> p kc n", p=128), reads=rd)
                for c in range(KC):
                    pb = bank()
                    for k in range(12):
                        mm(pb[:, :], wo[:, k, c * 128:(c + 1) * 128], mixT[:, k, tq * 512:(tq + 1) * 512],
                           start=(k == 0), stop=(k == 11))
                    tt('dve', xt[:, c, :], pb[:, :], xt[:, c, :], ALU.add)
                S.dma('sp', xs1_d[:, tq * 512:(tq + 1) * 512].rearrange("(kc p) n -> p kc n", p=128), xt,
                      writes=[dreg('xs1', tq * 512, (tq + 1) * 512)])

        def phase_mlp(layer, dst, dst_name, is_out):
            o = 0
            x1 = view(o, [128, KC, 1024], F32); o += 32768
            h2 = view(o, [128, KC, 1024], BF16); o += 16384
            hid = view(o, [128, 32, 1024], BF16); o += 65536
            w1b = [view(o + i * 2048, [128, KC, 128], BF16) for i in range(8)]; o += 16384
            w2b = [view(o + i * 8192, [128, 32, 128], BF16) for i in range(2)]; o += 16384
            sqb = [view(o + i * 2048, [128, 512], F32) for i in range(2)]; o += 4096
            tmp = o
            for st in range(2):
                t0 = st * 1024
                S.dma('sp', x1, xs1_d[:, t0:t0 + 1024].rearrange("(kc p) n -> p kc n", p=128),
                      reads=[dreg('xs1', t0, t0 + 1024)])
                fm_rmsnorm(x1, cols[:, C_N2 + layer * 8:C_N2 + layer * 8 + 8], h2, KC, 1024, D, tmp)
                for j in range(32):
                    wb = w1b[j % 8]
                    wload(wb, w1_d[layer, :, j * 128:(j + 1) * 128])
                    for hq in range(2):
                        pb = bank()
                        for kc in range(KC):
                            mm(pb[:, :], wb[:, kc, :], h2[:, kc, hq * 512:(hq + 1) * 512], start=(kc == 0), stop=(kc == KC - 1))
                        sq = sqb[(j * 2 + hq) % 2]
                        act(sq, pb[:, :], AF.Square)
                        stt(hid[:, j, hq * 512:(hq + 1) * 512], pb[:, :], 0.0, sq, ALU.is_gt, ALU.mult)
                for c in range(KC):
                    wb = w2b[c % 2]
                    S.dma('pool', wb, w2_d[layer, :, c * 128:(c + 1) * 128].rearrange("(j p) n -> p j n", p=128))
                    for hq in range(2):
                        pb = bank()
                        for j in range(32):
                            mm(pb[:, :], wb[:, j, :], hid[:, j, hq * 512:(hq + 1) * 512], start=(j == 0), stop=(j == 31))
                        tt('dve', x1[:, c, hq * 512:(hq + 1) * 512], pb[:, :], x1[:, c, hq * 512:(hq + 1) * 512], ALU.add)
                wr = [dreg(dst_name, t0, t0 + 1024)] if dst_name else []
                S.dma('sp', dst[:, t0:t0 + 1024].rearrange("(kc p) n -> p kc n", p=128), x1, writes=wr, is_output=is_out)

        def scan(out, data0, data1, initial, op0, op1):
            S.op('dve', lambda: nc.vector.tensor_tensor_scan(out=out, data0=data0, data1=data1, initial=initial,
                                                             op0=op0, op1=op1),
                 reads=[data0, data1], writes=[out])

        def phase_fox():
            W = foxw_d
            o = OFF_WORK
            Vtm = view(o, [128, 16, 1024], BF16); o += 32768
            o_head = o
            rowA = view(o, [128, T], F32)[0:8]
            rowB = view(o + 8192, [128, T], F32)[0:8]
            rowC = view(o + 16384, [128, T], F32)[0:8]
            rowH = view(o + 24576, [128, T], BF16)[0:8]
            wf = view(o + 28672, [128, KC, 8], BF16)
            wv = [view(o + 30720 + i * 8192, [128, KC, 512], BF16) for i in range(2)]
            for g in range(2):
                wload(wv[g], W[:, 2048 + g * 512:2048 + (g + 1) * 512])
            for g in range(2):
                for tl in range(16):
                    pb = bank()
                    for kc in range(KC):
                        mm(pb[:, :], hT[:, kc, tl * 128:(tl + 1) * 128], wv[g][:, kc, :], start=(kc == 0), stop=(kc == KC - 1))
                    copy('act' if tl % 2 else 'dve', Vtm[:, tl, g * 512:(g + 1) * 512], pb[:, :])
            wload(wf, W[:, 4096:4104])
            for tq in range(4):
                pb = bank()
                for kc in range(KC):
                    mm(pb[0:8, :], wf[:, kc, :], hT[:, kc, tq * 512:(tq + 1) * 512], start=(kc == 0), stop=(kc == KC - 1))
                act(rowA[:, tq * 512:(tq + 1) * 512], pb[0:8, :], AF.Identity, bias=cols[0:8, C_HS + 2:C_HS + 3])
            stt(rowB, rowA, -1.0, rowA, ALU.mult, ALU.min)
            act(rowB, rowB, AF.Exp)
            act(rowB, rowB, AF.Ln, bias=1.0)
            stt(rowA, rowA, 0.0, rowB, ALU.min, ALU.subtract)
            memset('dve', rowC, 1.0)
            scan(rowB, rowC, rowA, 0.0, ALU.mult, ALU.add)
            cur = rowB
            for part in range(3):
                copy('dve', rowH, cur)
                S.dma('sp', fsc_d[part], rowH, writes=[dreg('fsc', part, part + 1)])
                if part < 2:
                    nxt = rowA if part == 0 else rowC
                    tt('dve', nxt, cur, rowH, ALU.subtract)
                ts('dve', rowH, rowH, -1.0, ALU.mult)
                S.dma('sp', fsc_d[3 + part], rowH, writes=[dreg('fsc', 3 + part, 4 + part)])
                if part < 2:
                    cur = nxt
            o = o_head
            wq = [view(o + i * 2048, [128, KC, 128], BF16) for i in range(2)]; o += 4096
            qf = view(o, [128, 1, T], F32); o += 8192
            qn = [view(o + i * 4096, [128, 1, T], BF16) for i in range(2)]; o += 8192
            kn = [view(o + i * 4096, [128, 1, T], BF16) for i in range(2)]; o += 8192
            sg = [view(o + i * 4096, [128, T], BF16) for i in range(2)]; o += 8192
            Rq = [view(o + i * 4096, [128, T], BF16) for i in range(2)]; o += 8192
            Lk = [view(o + i * 4096, [128, T], BF16) for i in range(2)]; o += 8192
            pT = [view(o + i * 1024, [128, 512], BF16) for i in range(4)]; o += 4096
            rc = view(o, [128, 512], F32); o += 2048
            ot = view(o, [128, 512], F32); o += 2048
            tmp = o
            for i in range(2):
                memset('dve', Rq[i], 0.0)
                memset('dve', Lk[i], 0.0)
                memset('dve', Rq[i][0:6], 1.0)
                memset('dve', Lk[i][0:6], 1.0)
            wi = [0]
            pro_banks = [psb[6], psb[7]]

            def proj_fm(c0, dst_f32, sgt):
                wb = wq[wi[0] % 2]
                wi[0] += 1
                wload(wb, W[:, c0:c0 + 128])
                for tq in range(4):
                    pb = pro_banks[tq % 2]
                    for kc in range(KC):
                        mm(pb[:, :], wb[:, kc, :], hT[:, kc, tq * 512:(tq + 1) * 512], start=(kc == 0), stop=(kc == KC - 1))
                    yield
                    if dst_f32 is not None:
                        copy('act', dst_f32[:, 0, tq * 512:(tq + 1) * 512], pb[:, :])
                    else:
                        act(sgt[:, tq * 512:(tq + 1) * 512], pb[:, :], AF.Sigmoid)
                    yield

            def prologue(h):
                sl = h % 2
                for part in range(3):
                    S.dma('sp', Rq[sl][part:part + 1, :], fsc_d[part, h:h + 1, :], reads=[dreg('fsc', part, part + 1)])
                    S.dma('sp', Lk[sl][3 + part:4 + part, :], fsc_d[3 + part, h:h + 1, :], reads=[dreg('fsc', 3 + part, 4 + part)])
                yield from proj_fm(h * 128, qf, None)
                yield from fm_rmsnorm_gen(qf, cols[:, C_FOXQ:C_FOXQ + 1], qn[sl], 1, T, HD, tmp,
                                          post_ln_bias=float(np.log(HD ** -0.5)), pbank=pro_banks)
                yield from proj_fm(1024 + h * 128, qf, None)
                yield from fm_rmsnorm_gen(qf, cols[:, C_FOXK:C_FOXK + 1], kn[sl], 1, T, HD, tmp, pbank=pro_banks)
                yield from proj_fm(3072 + h * 128, None, sg[sl])

            def attention(h):
                sl = h % 2
                tiles = [(qc, kb) for qc in range(4) for kb in range(4 * qc + 4)]
                nt = len(tiles)

                def info(i):
                    qc, kb = tiles[i]
                    q0 = qc * 512
                    par = (h * 4 + qc) % 2
                    n0 = max(0, kb - 4 * qc) * 128
                    return qc, kb, q0, psb[par], psb[2 + par], n0, kb >= 4 * qc, 4 * qc + 4

                def stA(i):
                    qc, kb, q0, po, pl, n0, diag, nkb = info(i)
                    pss = psb[4 + i % 2]
                    mm(pss[:, n0:512], kn[sl][:, 0, kb * 128:(kb + 1) * 128], qn[sl][:, 0, q0 + n0:q0 + 512], start=True, stop=False)
                    mm(pss[:, n0:512], Lk[sl][:, kb * 128:(kb + 1) * 128], Rq[sl][:, q0 + n0:q0 + 512], start=False, stop=not diag)
                    if diag:
                        mm(pss[:, n0:n0 + 128], ident_bf, causal_bf, start=False, stop=True)

                def stB(i):
                    qc, kb, q0, po, pl, n0, diag, nkb = info(i)
                    act(pT[i % 4][:, n0:512], psb[4 + i % 2][:, n0:512], AF.Exp)

                def stC(i):
                    qc, kb, q0, po, pl, n0, diag, nkb = info(i)
                    p = pT[i % 4]
                    mm(po[:, n0:512], Vtm[:, kb, h * 128:(h + 1) * 128], p[:, n0:512], start=(kb == 0), stop=(kb == nkb - 1))
                    mm(pl[:, n0:512], ones_bf, p[:, n0:512], start=(kb == 0), stop=(kb == nkb - 1))
                    if kb == nkb - 1:
                        recip(rc, pl[:, :])
                        tt('dve', ot, po[:, :], rc, ALU.mult)
                        tt('dve', mixT[:, h, q0:q0 + 512], ot, sg[sl][:, q0:q0 + 512], ALU.mult)

                for i in range(nt + 2):
                    if i < nt:
                        stA(i)
                    if 0 <= i - 2 < nt:
                        stC(i - 2)
                    if 0 <= i - 1 < nt:
                        stB(i - 1)
                    yield

            for _ in prologue(0):
                pass
            for h in range(8):
                gens = [attention(h)]
                if h + 1 < 8:
                    gens.append(prologue(h + 1))
                while gens:
                    for g_ in list(gens):
                        try:
                            next(g_)
                        except StopIteration:
                            gens.remove(g_)

        def phase_dn():
            W = dnw_d
            ps_lim[0] = 7
            pcs = psb[7]
            o = OFF_WORK
            colq = view(o, [128, 4, 16, 8], F32); o += 2048
            egl = view(o, [128, 8, 32], F32); o += 1024
            cf = view(o, [128, 5, 2, 16], F32); o += 640
            wab = view(o, [128, KC, 16], BF16); o += 256
            nea = view(o, [128, 16], F32)[0:8]; o += 64
            id2 = view(o, [128, 2, 128], F32); o += 1024
            msk = [view(o + i * 512, [128, 2, 128], BF16) for i in range(3)]; o += 1536
            sc = view(o, [128, 64], F32); o += 256
            RR = [view(o + i * 4096, [128, T], BF16) for i in range(3)]; o += 12288
            for i in range(3):
                memset('dve', RR[i], 0.0)
            o_grp = o
            ident_f = cst[:, K_IDENT:K_IDENT + 128]
            for hl in range(2):
                copy('dve', id2[:, hl, :], ident_f)
                for i, kk_ in enumerate((K_MLOW, K_MUPS, K_MUPI)):
                    copy('dve', msk[i][:, hl, :], cst[:, kk_:kk_ + 128])
            bd6 = cbf[:, K_BD6:K_BD6 + 256]
            on6 = cbf[:, K_ON6:K_ON6 + 128]
            e6 = cst[0:6, K_E6:K_E6 + 2]
            selg = cbf[:, K_SELG:K_SELG + 4].rearrange("p (a b) -> p a b", a=2)
            onesrow = cst[0:8, K_ONES:K_ONES + 64]

            rA, rB, rC, rD, rE, rF = [view(o_grp + i * 8192, [128, T], F32)[0:8] for i in range(6)]
            wload(wab, W[:, 4096:4112])
            for tq in range(4):
                pb = bank()
                for kc in range(KC):
                    mm(pb[0:8, :], wab[:, kc, 0:8], hT[:, kc, tq * 512:(tq + 1) * 512], start=(kc == 0), stop=(kc == KC - 1))
                act(rA[:, tq * 512:(tq + 1) * 512], pb[0:8, :], AF.Identity, bias=cols[0:8, C_HS + 1:C_HS + 2])
                pb = bank()
                for kc in range(KC):
                    mm(pb[0:8, :], wab[:, kc, 8:16], hT[:, kc, tq * 512:(tq + 1) * 512], start=(kc == 0), stop=(kc == KC - 1))
                copy('act', rB[:, tq * 512:(tq + 1) * 512], pb[0:8, :])
            stt(rC, rA, -1.0, rA, ALU.mult, ALU.min)
            act(rC, rC, AF.Exp)
            act(rC, rC, AF.Ln, bias=1.0)
            stt(rA, rA, 0.0, rC, ALU.max, ALU.add)
            act(nea[:, 0:1], cols[0:8, C_HS:C_HS + 1], AF.Exp)
            ts('dve', nea[:, 0:1], nea[:, 0:1], -1.0, ALU.mult)
            ts('dve', rA, rA, nea[:, 0:1], ALU.mult)
            for c in range(32):
                scan(rD[:, c * 64:(c + 1) * 64], onesrow, rA[:, c * 64:(c + 1) * 64], 0.0, ALU.mult, ALU.add)
            stt(rC, rB, -1.0, rB, ALU.mult, ALU.min)
            act(rC, rC, AF.Exp)
            act(rC, rC, AF.Ln, bias=1.0)
            stt(rB, rB, 0.0, rC, ALU.min, ALU.subtract)
            tt('dve', rC, rD, rB, ALU.add)
            S.dma('sp', rows_d[0], rC, writes=[dreg('rowsd', 0, 1)])
            ts('dve', rE, rD, -1.0, ALU.mult)
            S.dma('sp', rows_d[1], rE, writes=[dreg('rowsd', 1, 2)])
            S.dma('sp', rows_d[2], rD, writes=[dreg('rowsd', 2, 3)])
            act(rB, rB, AF.Exp)
            act(rE, rD, AF.Exp)
            tt('dve', rC, rB, rE, ALU.mult)
            for c in range(32):
                ts('dve', rF[:, c * 64:(c + 1) * 64], rD[:, c * 64:(c + 1) * 64], -1.0, ALU.mult,
                   rD[:, c * 64 + 63:c * 64 + 64], ALU.add)
            act(rF, rF, AF.Exp)
            pcol = bank()
            for q, row in enumerate((rC, rB, rF, rE)):
                for tl in range(16):
                    cidx = (q * 16 + tl) * 8
                    mm(pcol[:, cidx:cidx + 8], row[0:8, tl * 128:(tl + 1) * 128], cst[0:8, K_IDENT:K_IDENT + 8])
            copy('dve', colq.rearrange("p a b c -> p (a b c)"), pcol[:, :])
            pe_ = bank()
            gl = rD.rearrange("p (c k) -> p c k", k=64)[:, :, 63]
            for h in range(8):
                mm(pe_[:, h * 32:(h + 1) * 32], cst[0:8, K_SEL8 + h * 128:K_SEL8 + (h + 1) * 128], gl)
            act(egl.rearrange("p a b -> p (a b)"), pe_[:, 0:256], AF.Exp)

            if dbg == 'dn_a':
                ps_lim[0] = 8
                return
            for g in range(4):
                if dbg is not None and dbg[:4] in ('dn_b', 'dn_c', 'dn_d', 'dn_e') and g > 0:
                    break
                o = o_grp
                QT = view(o, [128, 2, T], BF16); o += 8192
                KT = view(o, [128, 2, T], BF16); o += 8192
                VT = view(o, [128, 2, T], BF16); o += 8192
                R1 = view(o, [128, T], F32)[0:2]; o += 8192
                R2b = view(o, [128, T], F32)[0:2]; o += 8192
                R3 = view(o, [128, T], F32)[0:2]
                ssk = view(o, [128, T], F32)[0:2]; o += 8192
                o_conv = o
                Ppre2 = [view(o + i * 4112, [128, 2056], BF16) for i in range(2)]; o += 8224
                dg = [view(o + i * 1024, [128, 4, 128], BF16) for i in range(2)]; o += 2048
                wbf = [view(o + i * 2048, [128, KC, 128], BF16) for i in range(2)]; o += 4096
                sqt = [view(o + i * 1024, [128, 512], BF16) for i in range(2)]; o += 2048
                sgt = sqt
                o = o_conv
                exE = [[view(o + (j * 3 + i) * 1024, [128, 2, 128], F32) for i in range(3)] for j in range(2)]; o += 6144
                def nview(e0, shape, dt):
                    n_ = 1
                    for s_ in shape[1:]:
                        n_ *= s_
                    ap = nmn[:, e0:e0 + n_]
                    if dt != F32R:
                        ap = ap.bitcast(dt)
                    if len(shape) == 3:
                        return ap.rearrange("p (a b) -> p a b", a=shape[1])
                    return ap.rearrange("p (a b c) -> p a b c", a=shape[1], b=shape[2])
                Am = [[nview((j * 2 + i) * 256, [128, 2, 128], F32) for i in range(2)] for j in range(2)]
                AmR = [[nview((j * 2 + i) * 256, [128, 2, 128], F32R) for i in range(2)] for j in range(2)]
                BX = [[nview(1024 + (j * 2 + i) * 512, [128, 2, 2, 128], F32) for i in range(2)] for j in range(2)]
                BXR = [[nview(1024 + (j * 2 + i) * 512, [128, 2, 2, 128], F32R) for i in range(2)] for j in range(2)]
                BDt = [view(o + j * 1536, [128, 3, 2, 128], BF16) for j in range(2)]; o += 3072
                o_pc = o
                kbg = [view(o + i * 512, [128, 2, 128], BF16) for i in range(3)]; o += 1536
                kdec = [view(o + i * 512, [128, 2, 128], BF16) for i in range(3)]; o += 1536
                vb = [view(o + i * 512, [128, 2, 128], BF16) for i in range(3)]; o += 1536
                QKm = [view(o + i * 512, [128, 2, 128], BF16) for i in range(3)]; o += 1536
                TT = [view(o + i * 512, [128, 2, 128], BF16) for i in range(3)]; o += 1536
                wT = [view(o + i * 512, [128, 2, 128], BF16) for i in range(3)]; o += 1536
                u = [view(o + i * 1024, [128, 2, 128], F32) for i in range(3)]; o += 3072
                Sst = view(o, [128, 2, 128], F32); o += 1024
                Sbf = view(o, [128, 2, 128], BF16); o += 512
                vnew = view(o, [128, 2, 128], BF16); o += 512
                otm = view(o, [128, 2, 128], F32); o += 1024
                otmp = view(o, [128, 2, 128], F32); o += 1024
                osq = view(o, [128, 2, 128], F32); o += 1024
                onb = view(o, [128, 2, 128], BF16); o += 512
                ss = view(o, [128, 16], F32); o += 64

                for i in range(2):
                    memset('dve', Ppre2[i][:, 0:4], 0.0)
                memset('dve', ssk, 0.0)
                memset('dve', Sst, 0.0)
                memset('dve', Sbf, 0.0)
                wcnt = 0
                def row_section():
                    copy('dve', sc, pcs[:, 0:64])
                    act(sc, sc, AF.Ln, bias=EPS)
                    act(sc[:, 0:32], sc[:, 0:32], AF.Exp, scale=-0.5, bias=float(np.log(HD ** -0.5)))
                    act(sc[:, 32:64], sc[:, 32:64], AF.Exp, scale=-0.5)
                    rq = sc[:, 0:32].rearrange("p (a b) -> p a b", a=2)
                    rk = sc[:, 32:64].rearrange("p (a b) -> p a b", a=2)
                    for hl in range(2):
                        h = 2 * g + hl
                        tt('dve', cf[:, 0, hl, :], rk[:, hl, :], colq[:, 0, :, h], ALU.mult)
                        tt('dve', cf[:, 1, hl, :], rk[:, hl, :], colq[:, 2, :, h], ALU.mult)
                        tt('dve', cf[:, 2, hl, :], rq[:, hl, :], colq[:, 3, :, h], ALU.mult)
                        copy('dve', cf[:, 3, hl, :], rq[:, hl, :])
                        copy('dve', cf[:, 4, hl, :], colq[:, 1, :, h])
                    act(ssk, ssk, AF.Ln, bias=EPS)
                    ts('dve', ssk, ssk, -0.5, ALU.mult)
                    S.dma('sp', R1, rows_d[0, 2 * g:2 * g + 2, :], reads=[dreg('rowsd', 0, 1)])
                    S.dma('sp', R2b, rows_d[1, 2 * g:2 * g + 2, :], reads=[dreg('rowsd', 1, 2)])
                    tt('dve', R1, R1, ssk, ALU.add)
                    tt('dve', R2b, R2b, ssk, ALU.add)
                    S.dma('sp', R3, rows_d[2, 2 * g:2 * g + 2, :], reads=[dreg('rowsd', 2, 3)])
                    pcb = [view(o_conv + 16416, [128, T], BF16)[0:2], view(ARENA_BYTES - 4096, [128, T], BF16)[0:2]]
                    for ri, Rr in enumerate((R1, R2b, R3)):
                        for p in range(3):
                            pc = pcb[(ri * 3 + p) % 2]
                            copy('dve', pc, Rr)
                            S.dma('sp', RR[ri][2 * p:2 * p + 2, :], pc)
                            if p < 2:
                                tt('dve', Rr, Rr, pc, ALU.subtract)

                for kind, base, dst in ((0, 0, QT), (1, 1024, KT), (2, 2048, VT)):
                    if kind == 2:
                        row_section()
                    for hl in range(2):
                        h = 2 * g + hl
                        ch = kind * 8 + h
                        wbb = wbf[wcnt % 2]
                        dgt = dg[wcnt % 2]
                        Ppre = Ppre2[wcnt % 2]
                        wcnt += 1
                        wload(wbb, W[:, base + h * 128:base + (h + 1) * 128])
                        for tq in range(4):
                            pb = bank()
                            for kc in range(KC):
                                mm(pb[:, :], wbb[:, kc, :], hT[:, kc, tq * 512:(tq + 1) * 512], start=(kc == 0), stop=(kc == KC - 1))
                            copy('act' if (tq % 2 or kind == 2) else 'dve', Ppre[:, 4 + tq * 512:4 + (tq + 1) * 512], pb[:, :])
                        for j in range(4):
                            ts('pool', dgt[:, j, :], ident_bf, cols[:, C_CONV + j * 24 + ch:C_CONV + j * 24 + ch + 1], ALU.mult, 1.0, ALU.mult)
                        pending = None
                        for tq in range(4):
                            pb = bank()
                            for j in range(4):
                                mm(pb[:, :], dgt[:, j, :], Ppre[:, tq * 512 + 1 + j:tq * 512 + 1 + j + 512], start=(j == 0), stop=(j == 3))
                            dsl = dst[:, hl, tq * 512:(tq + 1) * 512]
                            act(dsl, pb[:, :], AF.Silu)
                            if pending is not None:
                                pending()
                                pending = None
                            if kind < 2:
                                sq = sqt[tq % 2]
                                tt('dve', sq, dsl, dsl, ALU.mult)

                                def pending(sq=sq, tq=tq):
                                    for t4 in range(4):
                                        col = (kind * 2 + hl) * 16 + tq * 4 + t4
                                        mm(pcs[:, col:col + 1], sq[:, t4 * 128:(t4 + 1) * 128], ones_bf[:, 0:1])
                                    if kind == 1:
                                        pr = bank()
                                        mm(pr[0:2, :], selg[:, hl, :], sq)
                                        tt('dve', ssk[:, tq * 512:(tq + 1) * 512], pr[0:2, :], ssk[:, tq * 512:(tq + 1) * 512], ALU.add)
                        if pending is not None:
                            pending()
                for j in range(2):
                    memset('dve', BDt[j], 0.0)

                def v3(ap):
                    return ap.rearrange("p (a b) -> p a b", a=2)

                def prep(n):
                    bs = n % 3
                    t2 = n % 2
                    exE_, Am_, BX_, BDt_ = exE[t2], Am[t2], BX[t2], BDt[t2]
                    AmR_, BXR_ = AmR[t2], BXR[t2]
                    b0, b1, b2 = psb[3 * t2], psb[3 * t2 + 1], psb[3 * t2 + 2]
                    tsl = slice(n * 128, (n + 1) * 128)
                    for i, ri in enumerate((1, 0, 2)):
                        for hl in range(2):
                            ts('pool', BDt_[0:6, i, hl, :], RR[ri][0:6, tsl], e6[:, hl:hl + 1], ALU.mult, 1.0, ALU.mult)
                    yield
                    ptb = b2[:].bitcast(BF16)[:, 512:1024]
                    for hl in range(2):
                        transpose(ptb[:, hl * 128:(hl + 1) * 128], KT[:, hl, tsl], ident_bf)
                        transpose(ptb[:, 256 + hl * 128:256 + (hl + 1) * 128], VT[:, hl, tsl], ident_bf)
                    pkk = b0[:, 0:256]
                    pqk = b0[:, 256:512]
                    for hl in range(2):
                        mm(pkk[:, hl * 128:(hl + 1) * 128], KT[:, hl, tsl], KT[:, hl, tsl])
                    for hl in range(2):
                        mm(pqk[:, hl * 128:(hl + 1) * 128], KT[:, hl, tsl], QT[:, hl, tsl])
                    specs = ((RR[0], 0, 0, b1[:, 0:256]), (RR[1], 1, 1, b1[:, 256:512]), (RR[1], 2, 2, b2[:, 0:256]))
                    pes = []
                    for lrow, bdi, mi, pe in specs:
                        mm(pe, lrow[:, tsl], bd6, start=True, stop=False)
                        mm(pe, on6, BDt_[:, bdi].rearrange("p a b -> p (a b)"), start=False, stop=False)
                        mm(pe, ident_bf, msk[mi].rearrange("p a b -> p (a b)"), start=False, stop=True)
                        pes.append(pe)
                    yield
                    for hl in range(2):
                        act(kbg[bs][:, hl, :], ptb[:, hl * 128:(hl + 1) * 128], AF.Copy, scale=cf[:, 0, hl, n:n + 1])
                        act(kdec[bs][:, hl, :], ptb[:, hl * 128:(hl + 1) * 128], AF.Copy, scale=cf[:, 1, hl, n:n + 1])
                        act(vb[bs][:, hl, :], ptb[:, 256 + hl * 128:256 + (hl + 1) * 128], AF.Copy, scale=cf[:, 4, hl, n:n + 1])
                    act(exE_[0], v3(pes[0]), AF.Exp)
                    act(exE_[1], v3(pes[1]), AF.Exp)
                    act(exE_[2], v3(pes[2]), AF.Exp)
                    yield
                    tt('dve', AmR_[0], v3(pkk), exE_[0], ALU.mult)
                    tt('dve', BXR_[0][:, :, 0, :], v3(pkk), exE_[1], ALU.mult)
                    stt(BXR_[1][:, :, 1, :], BX_[0][:, :, 0, :], -1.0, id2, ALU.mult, ALU.add)
                    tt('dve', QKm[bs], v3(pqk), exE_[2], ALU.mult)
                    yield
                    pa = b0
                    pbb = b1
                    for hl in range(2):
                        mm(pa[:, hl * 128:(hl + 1) * 128], BXR_[0][:, hl, 0, :], AmR_[0][:, hl, :])
                        mm(pbb[:, hl * 128:(hl + 1) * 128], AmR_[0][:, hl, :], BXR_[0][:, hl, 0, :])
                    yield
                    copy('act', AmR_[1], v3(pa[:, 0:256]))
                    copy('act', BXR_[1][:, :, 0, :], v3(pbb[:, 0:256]))
                    yield
                    cur = 1
                    for st in range(1, 6):
                        nx = 1 - cur
                        if st <= 3:
                            pbx = b2 if st % 2 else b1
                            pa = b0
                            for hl in range(2):
                                mm(pbx[:, hl * 256:(hl + 1) * 256], AmR_[cur][:, hl, :], BXR_[cur][:, hl, :, :].rearrange("p a b -> p (a b)"))
                                mm(pa[:, hl * 128:(hl + 1) * 128], BXR_[cur][:, hl, 0, :], AmR_[cur][:, hl, :])
                            yield
                            p4 = pbx[:, :].rearrange("p (h t c) -> p h t c", h=2, t=2)
                            copy('act', AmR_[nx], v3(pa[:, 0:256]))
                            tt('dve', BXR_[nx][:, :, 1, :], p4[:, :, 1, :], BX_[cur][:, :, 1, :], ALU.add)
                            copy('act', BXR_[nx][:, :, 0, :], p4[:, :, 0, :])
                            yield
                        elif st == 4:
                            pbx = b1
                            pa = b0
                            for hl in range(2):
                                mm(pbx[:, hl * 128:(hl + 1) * 128], AmR_[cur][:, hl, :], BXR_[cur][:, hl, 1, :])
                                mm(pa[:, hl * 128:(hl + 1) * 128], BXR_[cur][:, hl, 0, :], AmR_[cur][:, hl, :])
                            yield
                            copy('act', AmR_[nx], v3(pa[:, 0:256]))
                            tt('dve', BXR_[nx][:, :, 1, :], v3(pbx[:, 0:256]), BX_[cur][:, :, 1, :], ALU.add)
                            yield
                        else:
                            pbx = b2
                            for hl in range(2):
                                mm(pbx[:, hl * 128:(hl + 1) * 128], AmR_[cur][:, hl, :], BXR_[cur][:, hl, 1, :])
                            yield
                            tt('dve', TT[bs], v3(pbx[:, 0:256]), BX_[cur][:, :, 1, :], ALU.add)
                            yield
                        cur = nx
                    pu = b0
                    pw = b1
                    for hl in range(2):
                        mm(pu[:, hl * 128:(hl + 1) * 128], TT[bs][:, hl, :], vb[bs][:, hl, :])
                        mm(pw[:, hl * 128:(hl + 1) * 128], kbg[bs][:, hl, :], TT[bs][:, hl, :])
                    yield
                    copy('act', u[bs], v3(pu[:, 0:256]))
                    copy('dve', wT[bs], v3(pw[:, 0:256]))
                    yield

                def make_post(n):
                    pt2b = psb[6][:].bitcast(BF16)[:, 512:1024]

                    def p1():
                        tt('pool', osq, otm, otm, ALU.mult)
                        S.op('dve', lambda: nc.vector.tensor_reduce(out=ss[:, 0:2], in_=osq, axis=AX.X, op=ALU.add),
                             reads=[osq], writes=[ss[:, 0:2]])

                    def p2():
                        act(ss[:, 0:2], ss[:, 0:2], AF.Ln, scale=1.0 / HD, bias=EPS)
                        act(ss[:, 0:2], ss[:, 0:2], AF.Exp, scale=-0.5)

                    def p3():
                        for hl in range(2):
                            ts('pool', onb[:, hl, :], otm[:, hl, :], ss[:, hl:hl + 1], ALU.mult, 1.0, ALU.mult)

                    def p4():
                        for hl in range(2):
                            transpose(pt2b[:, hl * 128:(hl + 1) * 128], onb[:, hl, :], ident_bf)

                    def p5():
                        for hl in range(2):
                            act(mixT[:, 2 * g + hl, n * 128:(n + 1) * 128], pt2b[:, hl * 128:(hl + 1) * 128], AF.Copy,
                                scale=cols[:, C_DNO:C_DNO + 1])
                    return [p1, p2, p3, p4, p5]

                def scan_tile(n, post):
                    bs = n % 3
                    pend = None
                    for half in range(2):
                        cb = half * 64
                        c = 2 * n + half
                        ps_ = slice(cb, cb + 64)
                        pv = psb[6]
                        po = psb[7]
                        for hl in range(2):
                            mm(pv[ps_, hl * 128:(hl + 1) * 128], wT[bs][:, hl, cb:cb + 64], Sbf[:, hl, :])
                        if half == 0 and post is not None:
                            post[0]()
                        if pend is not None:
                            pend()
                            pend = None
                        for hl in range(2):
                            mm(po[ps_, hl * 256:hl * 256 + 128], QT[:, hl, n * 128 + cb:n * 128 + cb + 64], Sbf[:, hl, :])
                        if half == 1 and post is not None:
                            post[3]()
                        yield
                        tt('dve', vnew[ps_, :, :], u[bs][ps_, :, :], v3(pv[ps_, 0:256]), ALU.subtract)
                        if post is not None:
                            post[1 if half == 0 else 4]()
                        yield
                        pS = psb[6][:, 256:512]
                        for hl in range(2):
                            mm(pS[:, hl * 128:(hl + 1) * 128], kdec[bs][ps_, hl, :], vnew[ps_, hl, :])
                        for hl in range(2):
                            mm(po[ps_, hl * 256 + 128:hl * 256 + 256], QKm[bs][ps_, hl, cb:cb + 64], vnew[ps_, hl, :])
                        if half == 0 and post is not None:
                            post[2]()
                        yield
                        for hl in range(2):
                            h = 2 * g + hl
                            stt(Sbf[:, hl, :], Sst[:, hl, :], egl[:, h, c:c + 1], pS[:, hl * 128:(hl + 1) * 128], ALU.mult, ALU.add)
                        for hl in range(2):
                            act(otmp[ps_, hl, :], po[ps_, hl * 256:hl * 256 + 128], AF.Copy, scale=cf[ps_, 2, hl, n:n + 1])
                        for hl in range(2):
                            h = 2 * g + hl
                            stt(Sst[:, hl, :], Sst[:, hl, :], egl[:, h, c:c + 1], pS[:, hl * 128:(hl + 1) * 128], ALU.mult, ALU.add)
                        yield

                        def pend(ps_=ps_, po=po):
                            for hl in range(2):
                                stt(otm[ps_, hl, :], po[ps_, hl * 256 + 128:hl * 256 + 256], cf[ps_, 3, hl, n:n + 1], otmp[ps_, hl, :], ALU.mult, ALU.add)
                    pend()

                def drain(gen):
                    for _ in gen:
                        pass

                live = {}
                for n0 in (0, 1):
                    live[n0] = prep(n0)
                drain(live.pop(0))
                for n in range(16):
                    if n + 2 < 16:
                        live[n + 2] = prep(n + 2)
                    sc_ = scan_tile(n, make_post(n - 1) if n > 0 else None)
                    while True:
                        try:
                            next(sc_)
                        except StopIteration:
                            break
                        for k_ in sorted(live):
                            try:
                                next(live[k_])
                            except StopIteration:
                                del live[k_]
                    if n + 1 in live:
                        drain(live.pop(n + 1))
                for p_ in make_post(15):
                    p_()

                for hl in range(2):
                    h = 2 * g + hl
                    wbb = wbf[hl]
                    wload(wbb, W[:, 3072 + h * 128:3072 + (h + 1) * 128])
                    for tq in range(4):
                        pb = bank()
                        for kc in range(KC):
                            mm(pb[:, :], wbb[:, kc, :], hT[:, kc, tq * 512:(tq + 1) * 512], start=(kc == 0), stop=(kc == KC - 1))
                        sg_ = sgt[tq % 2]
                        act(sg_, pb[:, :], AF.Silu)
                        tt('dve', mixT[:, h, tq * 512:(tq + 1) * 512], mixT[:, h, tq * 512:(tq + 1) * 512], sg_, ALU.mult)
            ps_lim[0] = 8

        def phase_dn_stub():
            for h in range(8):
                memset('dve', mixT[:, h, :], 0.0)

        phase_mem()
        for li, layer in enumerate(layers):
            if li == 0:
                x_src, x_name = xT_d, None
            else:
                x_src, x_name = xs2_d, 'xs2'
            phase_norm1(layer, x_src, x_name)
            if layer == 0:
                if dbg == 'nodn':
                    phase_dn_stub()
                else:
                    phase_dn()
                if dbg is not None and dbg.startswith('dn'):
                    for k in range(8):
                        S.dma('sp', dbg_d[k * 128:(k + 1) * 128, :], mixT[:, k, :], is_output=True)
                    break
                phase_memattn(layer, dnw_d, 4 * 1024 + 16, OFF_WORK)
            else:
                phase_fox()
                phase_memattn(layer, foxw_d, 4 * 1024 + 8, OFF_WORK)
            if dbg == f'mix{layer}':
                for k in range(12):
                    S.dma('sp', dbg_d[k * 128:(k + 1) * 128, :], mixT[:, k, :], is_output=True)
            phase_outproj(layer, x_src, x_name)
            last = (li == len(layers) - 1)
            if last:
                phase_mlp(layer, outT_d, None, True)
            else:
                phase_mlp(layer, xs2_d, 'xs2', False)
        S.finish()
        print("instr counts", S.n_inst, "waits", S.n_wait)
    return nc


def host_pack(inputs):
    f = np.float32
    cols = np.zeros((128, NCOLS), f)

    def chunked(v):
        return np.ascontiguousarray(np.asarray(v, f).reshape(-1, 128).T)

    for l in range(2):
        cols[:, C_N1 + l * 8:C_N1 + l * 8 + 8] = chunked(inputs['norm1_w'][l])
        cols[:, C_N2 + l * 8:C_N2 + l * 8 + 8] = chunked(inputs['norm2_w'][l])
        cols[:, C_MEMQ + l] = inputs['memq_norm_w'][l]
    cols[:, C_MEMN:C_MEMN + 8] = chunked(inputs['mem_norm_w'])
    cols[:, C_MEMK] = inputs['mem_k_norm_w']
    cols[:, C_FOXQ] = inputs['fox_q_norm_w'][0]
    cols[:, C_FOXK] = inputs['fox_k_norm_w'][0]
    cols[:, C_DNO] = inputs['dn_o_norm_w'][0]
    for j in range(4):
        cols[:, C_CONV + j * 24:C_CONV + (j + 1) * 24] = chunked(inputs['dn_conv_w'][0][j])
    cols[0:8, C_HS + 0] = inputs['dn_a_log'][0]
    cols[0:8, C_HS + 1] = inputs['dn_dt_bias'][0]
    cols[0:8, C_HS + 2] = inputs['fox_f_bias'][0]

    consts = np.zeros((128, NCONST), f)
    consts[:, K_IDENT:K_IDENT + 128] = np.eye(128, dtype=f)
    kk, qq = np.meshgrid(np.arange(128), np.arange(128), indexing='ij')
    consts[:, K_CAUSAL:K_CAUSAL + 128] = np.where(kk <= qq, 0.0, NEG)
    consts[:, K_ONES:K_ONES + 128] = 1.0
    same = (kk // 64) == (qq // 64)
    consts[:, K_MLOW:K_MLOW + 128] = np.where(same & (kk > qq), 0.0, NEG)
    consts[:, K_MUPS:K_MUPS + 128] = np.where(same & (qq > kk), 0.0, NEG)
    consts[:, K_MUPI:K_MUPI + 128] = np.where(same & (qq >= kk), 0.0, NEG)
    sel8 = np.zeros((8, 8, 128), f)
    for h in range(8):
        sel8[h, h, :] = 1.0
    consts[0:8, K_SEL8:K_SEL8 + 1024] = sel8.reshape(8, 1024)
    selg = np.zeros((128, 2, 2), f)
    selg[:, 0, 0] = 1.0
    selg[:, 1, 1] = 1.0
    consts[:, K_SELG:K_SELG + 4] = selg.reshape(128, 4)
    bd6 = np.zeros((128, 2, 128), f)
    for p in range(3):
        for r in range(2):
            bd6[2 * p + r, r, :] = 1.0
            consts[2 * p + r, K_E6 + r] = 1.0
    consts[:, K_BD6:K_BD6 + 256] = bd6.reshape(128, 256)
    consts[0:6, K_ON6:K_ON6 + 128] = 1.0
    return cols, consts


_NC_CACHE = {}


def kernel(**inputs):
    inputs = {k: np.asarray(v) for k, v in inputs.items()}
    cols, consts = host_pack(inputs)
    if 'nc' not in _NC_CACHE:
        _NC_CACHE['nc'] = build_program()
    nc = _NC_CACHE['nc']
    shared = dict(
        cols=cols, consts=consts,
        w_mem_kv=np.ascontiguousarray(inputs['w_mem_kv'], dtype=np.float32),
        dn_w_in=np.ascontiguousarray(inputs['dn_w_in'][0], dtype=np.float32),
        fox_w_in=np.ascontiguousarray(inputs['fox_w_in'][0], dtype=np.float32),
        w_out=np.ascontiguousarray(inputs['w_out'], dtype=np.float32),
        w_mlp1=np.ascontiguousarray(inputs['w_mlp1'], dtype=np.float32),
        w_mlp2=np.ascontiguousarray(inputs['w_mlp2'], dtype=np.float32),
    )
    in_maps = []
    for b in range(N_CORES):
        m = dict(shared)
        m['xT'] = np.ascontiguousarray(inputs['x'][b].T, dtype=np.float32)
        m['memT'] = np.ascontiguousarray(inputs['mem'][b].T, dtype=np.float32)
        in_maps.append(m)
    res = run_bass_kernel_spmd(nc, in_maps, core_ids=list(range(N_CORES)))
    out = np.stack([np.ascontiguousarray(res.results[b]['outT'].T) for b in range(N_CORES)], axis=0)
    return out.astype(np.float32)
```

```python
import numpy as np
import ml_dtypes
import concourse.bass as bass
import concourse.mybir as mybir
from concourse.bass_utils import run_bass_kernel_spmd
from contextlib import ExitStack

F32 = mybir.dt.float32
F32R = mybir.dt.float32r
BF16 = mybir.dt.bfloat16
AF = mybir.ActivationFunctionType
ALU = mybir.AluOpType
AX = mybir.AxisListType

T = 2048
D = 1024
KC = 8
NMEM = 256
HD = 128
DN_IN = 4624
FOX_IN = 4616
DFF = 4096
EPS = 1e-6
NEG = -30000.0
N_CORES = 8

C_N1 = 0
C_N2 = 16
C_MEMN = 32
C_MEMK = 40
C_MEMQ = 41
C_FOXQ = 43
C_FOXK = 44
C_DNO = 45
C_CONV = 46
C_HS = 142
NCOLS = 148

K_IDENT = 0
K_CAUSAL = 128
K_ONES = 256
K_SELG = 384
K_BD6 = 388
K_ON6 = 644
NBF = 772
K_MLOW = 772
K_MUPS = 900
K_MUPI = 1028
K_SEL8 = 1156
K_E6 = 2180
NCONST = 2184


def _dt_bytes(dt):
    s = str(dt)
    if '32' in s:
        return 4
    if '16' in s:
        return 2
    if '64' in s:
        return 8
    if '8' in s:
        return 1
    raise ValueError(s)


class Sched:
    SEM_LIMIT = 30000

    def __init__(self, nc, es):
        self.nc = nc
        self.es = es
        self.eng = {'pe': nc.tensor, 'dve': nc.vector, 'act': nc.scalar, 'pool': nc.gpsimd, 'sp': nc.sync}
        self.esem = {}
        self.ecnt = {}
        self.eepoch = {}
        for e in ['pe', 'dve', 'act', 'pool']:
            self._new_epoch(e)
        self.waited = {e: {} for e in self.eng}
        self.regions = {}
        self.dq = {}
        for q, n in (('sp', 16), ('pool', 16), ('act', 6)):
            self.dq[q] = dict(sems=[es.enter_context(nc.semaphore(f"dma_{q}{i}")) for i in range(n)],
                              cnt=[0] * n, nxt=0)
        self.out_events = []
        self.n_inst = {e: 0 for e in self.eng}
        self.n_wait = {e: 0 for e in self.eng}

    def _new_epoch(self, e):
        ep = self.eepoch.get(e, -1) + 1
        self.eepoch[e] = ep
        self.esem[e] = self.es.enter_context(self.nc.semaphore(f"e_{e}_{ep}"))
        self.ecnt[e] = 0

    @staticmethod
    def region(ap):
        a = ap.ap
        pstride, pcount = a[0]
        off = ap.offset
        if pstride > 0:
            p0 = off // pstride
            f0 = off % pstride
        else:
            p0 = 0
            f0 = off
        ext = 1
        for st, cnt in a[1:]:
            ext += (cnt - 1) * abs(st)
        b = _dt_bytes(ap.dtype)
        return (ap.tensor.name, p0, p0 + pcount, f0 * b, (f0 + ext) * b)

    def _regs(self, aps):
        out = []
        for a in aps:
            if a is None:
                continue
            if isinstance(a, tuple):
                out.append(a)
            else:
                out.append(self.region(a))
        return out

    def _deps(self, reads, writes):
        deps = []
        for name, p0, p1, f0, f1 in reads:
            for ent in self.regions.get(name, ()):
                if ent[4] == 'w' and ent[0] < p1 and p0 < ent[1] and ent[2] < f1 and f0 < ent[3]:
                    deps.append(ent[5])
        for name, p0, p1, f0, f1 in writes:
            for ent in self.regions.get(name, ()):
                if ent[0] < p1 and p0 < ent[1] and ent[2] < f1 and f0 < ent[3]:
                    deps.append(ent[5])
        return deps

    def _record(self, reads, writes, ev):
        for name, p0, p1, f0, f1 in writes:
            lst = self.regions.setdefault(name, [])
            lst[:] = [e for e in lst if not (p0 <= e[0] and e[1] <= p1 and f0 <= e[2] and e[3] <= f1)]
            lst.append((p0, p1, f0, f1, 'w', ev))
        for name, p0, p1, f0, f1 in reads:
            lst = self.regions.setdefault(name, [])
            for i, e in enumerate(lst):
                if e[4] == 'r' and e[0] == p0 and e[1] == p1 and e[2] == f0 and e[3] == f1 and e[5][0] is ev[0]:
                    lst[i] = (p0, p1, f0, f1, 'r', ev)
                    break
            else:
                lst.append((p0, p1, f0, f1, 'r', ev))

    def _wait(self, e, deps, skip_sem=None):
        need = {}
        for sem, val in deps:
            if sem is skip_sem:
                continue
            k = id(sem)
            if self.waited[e].get(k, 0) >= val:
                continue
            if k not in need or need[k][1] < val:
                need[k] = (sem, val)
        for k, (sem, val) in need.items():
            self.eng[e].wait_ge(sem, val)
            self.waited[e][k] = val
            self.n_wait[e] += 1

    def op(self, e, fn, reads=(), writes=(), inc=True):
        reads = self._regs(reads)
        writes = self._regs(writes)
        if e == 'pe':
            writes = [(nm, (p0 // 32) * 32, -(-p1 // 32) * 32, 0, 2048) for nm, p0, p1, f0, f1 in writes]
        deps = self._deps(reads, writes)
        if e != 'pe':
            own = self.esem[e]
            for nm, p0, p1, f0, f1 in list(reads) + list(writes):
                if nm.startswith('ps'):
                    for ent in self.regions.get(nm, ()):
                        if ent[5][0] is not own:
                            deps.append(ent[5])
        self._wait(e, deps, skip_sem=(self.esem['pe'] if e == 'pe' else None))
        inst = fn()
        self.n_inst[e] += 1
        if inc:
            self.ecnt[e] += 1
            inst.then_inc(self.esem[e], 1)
            ev = (self.esem[e], self.ecnt[e])
        else:
            assert e == 'pe'
            ev = (self.esem[e], self.ecnt[e] + 1)
        self._record(reads, writes, ev)
        if inc and self.ecnt[e] >= self.SEM_LIMIT:
            self._new_epoch(e)
        return ev

    def dma(self, q, out, in_, reads=(), writes=(), is_output=False, **kw):
        rr = list(reads)
        ww = list(writes)
        if str(in_.space) != 'DRAM':
            rr.append(in_)
        if str(out.space) != 'DRAM':
            ww.append(out)
        rr = self._regs(rr)
        ww = self._regs(ww)
        deps = self._deps(rr, ww)
        dq = self.dq[q]
        i = dq['nxt']
        dq['nxt'] = (i + 1) % len(dq['sems'])
        sem = dq['sems'][i]
        if dq['cnt'][i] > 0:
            deps.append((sem, dq['cnt'][i]))
        self._wait(q, deps)
        dq['cnt'][i] += 16
        self.eng[q].dma_start(out=out, in_=in_, **kw).then_inc(sem, 16)
        self.n_inst[q] += 1
        ev = (sem, dq['cnt'][i])
        self._record(rr, ww, ev)
        if is_output:
            self.out_events.append(ev)
        return ev

    def finish(self):
        self._wait('sp', self.out_events)


def dreg(name, t0, t1):
    return (name, 0, 1, t0, t1)


def build_program(layers=(0, 1), dbg=None):
    nc = bass.Bass("TRN2", target_bir_lowering=False)

    def dram(name, shape, dt=F32, kind="ExternalInput"):
        return nc.dram_tensor(name, shape, dt, kind=kind).ap()

    xT_d = dram("xT", [D, T])
    memT_d = dram("memT", [D, NMEM])
    cols_d = dram("cols", [128, NCOLS])
    consts_d = dram("consts", [128, NCONST])
    wmemkv_d = dram("w_mem_kv", [D, 1024])
    dnw_d = dram("dn_w_in", [D, DN_IN])
    foxw_d = dram("fox_w_in", [D, FOX_IN])
    wout_d = dram("w_out", [2, 1536, D])
    w1_d = dram("w_mlp1", [2, D, DFF])
    w2_d = dram("w_mlp2", [2, DFF, D])
    outT_d = dram("outT", [D, T], kind="ExternalOutput")
    xs1_d = dram("xs1", [D, T], kind="Internal")
    xs2_d = dram("xs2", [D, T], kind="Internal")
    fsc_d = dram("fsc", [6, 8, T], BF16, kind="Internal")
    rows_d = dram("rowsd", [3, 8, T], F32, kind="Internal")
    dbg_d = None
    if dbg is not None:
        dbg_d = dram("dbg", [12 * 128, T], BF16, kind="ExternalOutput")

    es = ExitStack()
    with es:
        S = Sched(nc, es)
        ARENA_BYTES = 178 * 1024
        arena = es.enter_context(nc.sbuf_tensor("arena", [128, ARENA_BYTES // 4], F32))
        cols = es.enter_context(nc.sbuf_tensor("colsb", [128, NCOLS], F32))
        cst = es.enter_context(nc.sbuf_tensor("cst", [128, NCONST], F32))
        cbf = es.enter_context(nc.sbuf_tensor("cbf", [128, NBF], BF16))
        nmn = es.enter_context(nc.sbuf_tensor("nmn", [128, 3072], F32R))
        memKT = es.enter_context(nc.sbuf_tensor("memKT", [128, 4, NMEM], BF16))
        memV = es.enter_context(nc.sbuf_tensor("memV", [128, 2, 512], BF16))
        psb = [es.enter_context(nc.psum_tensor(f"ps{i}", [128, 512], F32)) for i in range(8)]

        def view(off, shape, dt=F32):
            n = 1
            for s in shape[1:]:
                n *= s
            nb = n * _dt_bytes(dt)
            assert off % 4 == 0 and nb % 4 == 0 and off + nb <= ARENA_BYTES, (off, nb)
            ap = arena[0:128, off // 4:(off + nb) // 4]
            if dt != F32:
                ap = ap.bitcast(dt)
            if len(shape) == 3:
                ap = ap.rearrange("p (a b) -> p a b", a=shape[1])
            elif len(shape) == 4:
                ap = ap.rearrange("p (a b c) -> p a b c", a=shape[1], b=shape[2])
            if shape[0] != 128:
                ap = ap[0:shape[0]]
            return ap

        ps_rr = [0]
        ps_lim = [8]

        def bank(i=None):
            if i is None:
                i = ps_rr[0] % ps_lim[0]
                ps_rr[0] = (i + 1) % ps_lim[0]
            return psb[i]

        def is_ap(x):
            return not isinstance(x, (int, float)) and x is not None

        def mm(out, lhsT, rhs, start=True, stop=True):
            S.op('pe', lambda: nc.tensor.matmul(out, lhsT=lhsT, rhs=rhs, start=start, stop=stop),
                 reads=[lhsT, rhs], writes=[out], inc=stop)

        def transpose(out, in_, ident):
            S.op('pe', lambda: nc.tensor.transpose(out, in_, ident), reads=[in_, ident], writes=[out])

        def act(out, in_, func, scale=1.0, bias=None, accum=None):
            reads = [in_]
            kw = dict(out=out, in_=in_, func=func, scale=scale)
            if is_ap(scale):
                reads.append(scale)
            if bias is not None:
                kw['bias'] = bias
                if is_ap(bias):
                    reads.append(bias)
            writes = [out]
            if accum is not None:
                kw['accum_out'] = accum
                writes.append(accum)
            S.op('act', lambda: nc.scalar.activation(**kw), reads=reads, writes=writes)

        def ts(e, out, in0, s1, op0, s2=None, op1=None):
            reads = [in0] + [s for s in (s1, s2) if is_ap(s)]
            eng = nc.vector if e == 'dve' else nc.gpsimd
            kw = dict(out=out, in0=in0, scalar1=s1, scalar2=s2, op0=op0)
            if op1 is not None:
                kw['op1'] = op1
            S.op(e, lambda: eng.tensor_scalar(**kw), reads=reads, writes=[out])

        def tt(e, out, in0, in1, op):
            eng = nc.vector if e == 'dve' else nc.gpsimd
            S.op(e, lambda: eng.tensor_tensor(out=out, in0=in0, in1=in1, op=op), reads=[in0, in1], writes=[out])

        def stt(out, in0, scalar, in1, op0, op1):
            reads = [in0, in1] + ([scalar] if is_ap(scalar) else [])
            S.op('dve', lambda: nc.vector.scalar_tensor_tensor(out=out, in0=in0, scalar=scalar, in1=in1, op0=op0, op1=op1),
                 reads=reads, writes=[out])

        def copy(e, out, in_):
            if e == 'act':
                S.op('act', lambda: nc.scalar.copy(out=out, in_=in_), reads=[in_], writes=[out])
            else:
                eng = nc.vector if e == 'dve' else nc.gpsimd
                S.op(e, lambda: eng.tensor_copy(out=out, in_=in_), reads=[in_], writes=[out])

        def memset(e, out, val):
            eng = nc.vector if e == 'dve' else nc.gpsimd
            S.op(e, lambda: eng.memset(out, val), writes=[out])

        def recip(out, in_):
            S.op('dve', lambda: nc.vector.reciprocal(out=out, in_=in_), reads=[in_], writes=[out])

        def wload(dst, src_rows_cols):
            S.dma('pool', dst, src_rows_cols.rearrange("(kc p) n -> p kc n", p=128))

        S.dma('sp', cols[:], cols_d)
        S.dma('sp', cst[:], consts_d)
        copy('dve', cbf[:], cst[:, 0:NBF])
        ident_bf = cbf[:, K_IDENT:K_IDENT + 128]
        causal_bf = cbf[:, K_CAUSAL:K_CAUSAL + 128]
        ones_bf = cbf[:, K_ONES:K_ONES + 128]

        def fm_rmsnorm_gen(src, wcol, dst, kcn, tn, dn, tmp_off, post_ln_bias=0.0, pbank=None, fine=False):
            sq = view(tmp_off, [128, kcn, 512], BF16)
            rs = view(tmp_off + kcn * 1024, [128, 512], F32)
            for t0 in range(0, tn, 512):
                tw = min(512, tn - t0)
                pb = bank() if pbank is None else (pbank[(t0 // 512) % len(pbank)] if isinstance(pbank, list) else pbank)
                for kc in range(kcn):
                    act(sq[:, kc, :tw], src[:, kc, t0:t0 + tw], AF.Square)
                if fine:
                    yield
                for kc in range(kcn):
                    mm(pb[:, :tw], ones_bf, sq[:, kc, :tw], start=(kc == 0), stop=(kc == kcn - 1))
                yield
                act(rs[:, :tw], pb[:, :tw], AF.Ln, scale=1.0 / dn, bias=EPS)
                act(rs[:, :tw], rs[:, :tw], AF.Exp, scale=-0.5, bias=post_ln_bias)
                for kc in range(kcn):
                    stt(dst[:, kc, t0:t0 + tw], src[:, kc, t0:t0 + tw], wcol[:, kc:kc + 1], rs[:, :tw],
                        ALU.mult, ALU.mult)
                yield

        def fm_rmsnorm(*a, **k):
            for _ in fm_rmsnorm_gen(*a, **k):
                pass

        OFF_HT = 0
        OFF_MIX = 32 * 1024
        OFF_WORK = 80 * 1024
        hT = view(OFF_HT, [128, KC, T], BF16)
        mixT = view(OFF_MIX, [128, 12, T], BF16)

        def phase_mem():
            o = OFF_WORK
            mT = view(o, [128, KC, NMEM], F32); o += KC * NMEM * 4
            mn = view(o, [128, KC, NMEM], BF16); o += KC * NMEM * 2
            wkv = view(o, [128, KC, 1024], BF16); o += KC * 1024 * 2
            kf = view(o, [128, 1, NMEM], F32); o += NMEM * 4
            tmp = o
            S.dma('sp', mT, memT_d.rearrange("(kc p) n -> p kc n", p=128))
            wload(wkv[:, :, 0:512], wmemkv_d[:, 0:512])
            wload(wkv[:, :, 512:1024], wmemkv_d[:, 512:1024])
            fm_rmsnorm(mT, cols[:, C_MEMN:C_MEMN + 8], mn, KC, NMEM, D, tmp)
            for h in range(4):
                pb = bank()
                for kc in range(KC):
                    mm(pb[:, :NMEM], wkv[:, kc, h * 128:(h + 1) * 128], mn[:, kc, :], start=(kc == 0), stop=(kc == KC - 1))
                copy('dve', kf[:, 0, :], pb[:, :NMEM])
                fm_rmsnorm(kf, cols[:, C_MEMK:C_MEMK + 1], memKT[:, h:h + 1, :], 1, NMEM, HD, tmp)
            for mt in range(2):
                pb = bank()
                for kc in range(KC):
                    mm(pb[:, :], mn[:, kc, mt * 128:(mt + 1) * 128], wkv[:, kc, 512:1024], start=(kc == 0), stop=(kc == KC - 1))
                copy('act', memV[:, mt, :], pb[:, :])

        def phase_norm1(layer, x_src, x_src_name):
            o = OFF_WORK
            xb = [view(o + i * 16384, [128, KC, 512], F32) for i in range(2)]
            tmp = o + 32768
            for tq in range(4):
                xt = xb[tq % 2]
                rd = [dreg(x_src_name, tq * 512, (tq + 1) * 512)] if x_src_name else []
                S.dma('sp', xt, x_src[:, tq * 512:(tq + 1) * 512].rearrange("(kc p) n -> p kc n", p=128), reads=rd)
                fm_rmsnorm(xt, cols[:, C_N1 + layer * 8:C_N1 + layer * 8 + 8], hT[:, :, tq * 512:(tq + 1) * 512],
                           KC, 512, D, tmp)

        def phase_memattn(layer, w_in_d, qm_off, work_off):
            o = work_off
            wq = [view(o + i * 2048, [128, KC, 128], BF16) for i in range(2)]; o += 4096
            qf = view(o, [128, 1, T], F32); o += T * 4
            qn = [view(o + i * T * 2, [128, 1, T], BF16) for i in range(2)]; o += T * 4
            pT = [view(o + i * 1024, [128, 512], BF16) for i in range(4)]; o += 4096
            rc = view(o, [128, 512], F32); o += 2048
            tmp = o
            pro_bank = psb[7]

            def prologue(h):
                c0 = qm_off + h * 128
                wb = wq[h % 2]
                wload(wb, w_in_d[:, c0:c0 + 128])
                for tq in range(4):
                    for kc in range(KC):
                        mm(pro_bank[:, :], wb[:, kc, :], hT[:, kc, tq * 512:(tq + 1) * 512], start=(kc == 0), stop=(kc == KC - 1))
                    yield
                    copy('act', qf[:, 0, tq * 512:(tq + 1) * 512], pro_bank[:, :])
                    yield
                yield from fm_rmsnorm_gen(qf, cols[:, C_MEMQ + layer:C_MEMQ + layer + 1], qn[h % 2], 1, T, HD, tmp,
                                          pbank=pro_bank)

            def attention(h):
                qh = qn[h % 2]

                def stA(i):
                    tq, mt = divmod(i, 2)
                    mm(psb[4 + i % 3][:, :], memKT[:, h, mt * 128:(mt + 1) * 128], qh[:, 0, tq * 512:(tq + 1) * 512])

                def stB(i):
                    act(pT[i % 4], psb[4 + i % 3][:, :], AF.Exp, scale=float(HD) ** -0.5)

                def stC(i):
                    tq, mt = divmod(i, 2)
                    par = (h * 4 + tq) % 2
                    po, pl = psb[par], psb[2 + par]
                    p = pT[i % 4]
                    mm(po[:, :], memV[:, mt, h * 128:(h + 1) * 128], p, start=(mt == 0), stop=(mt == 1))
                    mm(pl[:, :], ones_bf, p, start=(mt == 0), stop=(mt == 1))
                    if mt == 1:
                        recip(rc, pl[:, :])
                        tt('dve', mixT[:, 8 + h, tq * 512:(tq + 1) * 512], po[:, :], rc, ALU.mult)

                for i in range(8 + 2):
                    if i < 8:
                        stA(i)
                    if 0 <= i - 2 < 8:
                        stC(i - 2)
                    if 0 <= i - 1 < 8:
                        stB(i - 1)
                    yield

            for _ in prologue(0):
                pass
            for h in range(4):
                gens = [attention(h)]
                if h + 1 < 4:
                    gens.append(prologue(h + 1))
                while gens:
                    for g_ in list(gens):
                        try:
                            next(g_)
                        except StopIteration:
                            gens.remove(g_)

        def phase_outproj(layer, x_src, x_src_name):
            wo = view(OFF_HT, [128, 12, 1024], BF16)
            o = OFF_WORK
            xb = [view(o + i * 16384, [128, KC, 512], F32) for i in range(2)]
            wload(wo[:, :, 0:512], wout_d[layer, :, 0:512])
            wload(wo[:, :, 512:1024], wout_d[layer, :, 512:1024])
            for tq in range(4):
                xt = xb[tq % 2]
                rd = [dreg(x_src_name, tq * 512, (tq + 1) * 512)] if x_src_name else []
                S.dma('sp', xt, x_src[:, tq * 512:(tq + 1) * 512].rearrange("(kc p) n -> p kc n", p=128), reads=rd)
                for c in range(KC):
                    pb = bank()
                    for k in range(12):
                        mm(pb[:, :], wo[:, k, c * 128:(c + 1) * 128], mixT[:, k, tq * 512:(tq + 1) * 512],
                           start=(k == 0), stop=(k == 11))
                    tt('dve', xt[:, c, :], pb[:, :], xt[:, c, :], ALU.add)
                S.dma('sp', xs1_d[:, tq * 512:(tq + 1) * 512].rearrange("(kc p) n -> p kc n", p=128), xt,
                      writes=[dreg('xs1', tq * 512, (tq + 1) * 512)])

        def phase_mlp(layer, dst, dst_name, is_out):
            o = 0
            x1s = [view(o + i * 32768, [128, KC, 1024], F32) for i in range(2)]; o += 65536
            h2 = view(o, [128, KC, 1024], BF16); o += 16384
            hid = view(o, [128, 32, 1024], BF16); o += 65536
            w1b = [view(o + i * 2048, [128, KC, 128], BF16) for i in range(4)]; o += 8192
            w2b = [view(o + i * 8192, [128, 32, 128], BF16) for i in range(2)]; o += 16384
            sqb = [view(o + i * 2048, [128, 512], F32) for i in range(2)]
            tmp = o
            ps_lim[0] = 7
            ncol = cols[:, C_N2 + layer * 8:C_N2 + layer * 8 + 8]
            for st in range(2):
                t0 = st * 1024
                S.dma('sp', x1s[st], xs1_d[:, t0:t0 + 1024].rearrange("(kc p) n -> p kc n", p=128),
                      reads=[dreg('xs1', t0, t0 + 1024)])
            fm_rmsnorm(x1s[0], ncol, h2, KC, 1024, D, tmp, pbank=psb[7])
            for st in range(2):
                t0 = st * 1024
                x1 = x1s[st]
                for j in range(32):
                    wb = w1b[j % 4]
                    wload(wb, w1_d[layer, :, j * 128:(j + 1) * 128])
                    for hq in range(2):
                        pb = bank()
                        for kc in range(KC):
                            mm(pb[:, :], wb[:, kc, :], h2[:, kc, hq * 512:(hq + 1) * 512], start=(kc == 0), stop=(kc == KC - 1))
                        sq = sqb[(j * 2 + hq) % 2]
                        act(sq, pb[:, :], AF.Square)
                        stt(hid[:, j, hq * 512:(hq + 1) * 512], pb[:, :], 0.0, sq, ALU.is_gt, ALU.mult)
                ngen = None
                if st + 1 < 2:
                    ngen = fm_rmsnorm_gen(x1s[st + 1], ncol, h2, KC, 1024, D, tmp, pbank=psb[7], fine=True)
                for c in range(KC):
                    wb = w2b[c % 2]
                    S.dma('pool', wb, w2_d[layer, :, c * 128:(c + 1) * 128].rearrange("(j p) n -> p j n", p=128))
                    for hq in range(2):
                        pb = bank()
                        for j in range(32):
                            mm(pb[:, :], wb[:, j, :], hid[:, j, hq * 512:(hq + 1) * 512], start=(j == 0), stop=(j == 31))
                        tt('dve', x1[:, c, hq * 512:(hq + 1) * 512], pb[:, :], x1[:, c, hq * 512:(hq + 1) * 512], ALU.add)
                        if ngen is not None:
                            try:
                                next(ngen)
                            except StopIteration:
                                ngen = None
                if ngen is not None:
                    for _ in ngen:
                        pass
                wr = [dreg(dst_name, t0, t0 + 1024)] if dst_name else []
                S.dma('sp', dst[:, t0:t0 + 1024].rearrange("(kc p) n -> p kc n", p=128), x1, writes=wr, is_output=is_out)
            ps_lim[0] = 8

        def scan(out, data0, data1, initial, op0, op1):
            S.op('dve', lambda: nc.vector.tensor_tensor_scan(out=out, data0=data0, data1=data1, initial=initial,
                                                             op0=op0, op1=op1),
                 reads=[data0, data1], writes=[out])

        def phase_fox():
            W = foxw_d
            o = OFF_WORK
            Vtm = view(o, [128, 16, 1024], BF16); o += 32768
            o_head = o
            rowA = view(o, [128, T], F32)[0:8]
            rowB = view(o + 8192, [128, T], F32)[0:8]
            rowC = view(o + 16384, [128, T], F32)[0:8]
            rowH = view(o + 24576, [128, T], BF16)[0:8]
            wf = view(o + 28672, [128, KC, 8], BF16)
            wv = [view(o + 30720 + i * 8192, [128, KC, 512], BF16) for i in range(2)]
            for g in range(2):
                wload(wv[g], W[:, 2048 + g * 512:2048 + (g + 1) * 512])
            for g in range(2):
                for tl in range(16):
                    pb = bank()
                    for kc in range(KC):
                        mm(pb[:, :], hT[:, kc, tl * 128:(tl + 1) * 128], wv[g][:, kc, :], start=(kc == 0), stop=(kc == KC - 1))
                    copy('act' if tl % 2 else 'dve', Vtm[:, tl, g * 512:(g + 1) * 512], pb[:, :])
            wload(wf, W[:, 4096:4104])
            for tq in range(4):
                pb = bank()
                for kc in range(KC):
                    mm(pb[0:8, :], wf[:, kc, :], hT[:, kc, tq * 512:(tq + 1) * 512], start=(kc == 0), stop=(kc == KC - 1))
                act(rowA[:, tq * 512:(tq + 1) * 512], pb[0:8, :], AF.Identity, bias=cols[0:8, C_HS + 2:C_HS + 3])
            stt(rowB, rowA, -1.0, rowA, ALU.mult, ALU.min)
            act(rowB, rowB, AF.Exp)
            act(rowB, rowB, AF.Ln, bias=1.0)
            stt(rowA, rowA, 0.0, rowB, ALU.min, ALU.subtract)
            memset('dve', rowC, 1.0)
            scan(rowB, rowC, rowA, 0.0, ALU.mult, ALU.add)
            cur = rowB
            for part in range(3):
                copy('dve', rowH, cur)
                S.dma('sp', fsc_d[part], rowH, writes=[dreg('fsc', part, part + 1)])
                if part < 2:
                    nxt = rowA if part == 0 else rowC
                    tt('dve', nxt, cur, rowH, ALU.subtract)
                ts('dve', rowH, rowH, -1.0, ALU.mult)
                S.dma('sp', fsc_d[3 + part], rowH, writes=[dreg('fsc', 3 + part, 4 + part)])
                if part < 2:
                    cur = nxt
            o = o_head
            wq = [view(o + i * 2048, [128, KC, 128], BF16) for i in range(3)]; o += 6144
            qf = view(o, [128, 1, T], F32); o += 8192
            qn = [view(o + i * 4096, [128, 1, T], BF16) for i in range(2)]; o += 8192
            kn = [view(o + i * 4096, [128, 1, T], BF16) for i in range(2)]; o += 8192
            sg = [view(o + i * 4096, [128, T], BF16) for i in range(2)]; o += 8192
            Rq = [view(o + i * 4096, [128, T], BF16) for i in range(2)]; o += 8192
            Lk = [view(o + i * 4096, [128, T], BF16) for i in range(2)]; o += 8192
            pT = [view(o + i * 1024, [128, 512], BF16) for i in range(4)]; o += 4096
            rc = view(o, [128, 512], F32); o += 2048
            ot = view(o, [128, 512], F32); o += 2048
            tmp = o
            for i in range(2):
                memset('dve', Rq[i], 0.0)
                memset('dve', Lk[i], 0.0)
                memset('dve', Rq[i][0:6], 1.0)
                memset('dve', Lk[i][0:6], 1.0)
            wi = [0]
            pro_banks = [psb[6], psb[7]]

            def proj_fm(wb, dst_f32, sgt):
                for tq in range(4):
                    pb = pro_banks[tq % 2]
                    for kc in range(KC):
                        mm(pb[:, :], wb[:, kc, :], hT[:, kc, tq * 512:(tq + 1) * 512], start=(kc == 0), stop=(kc == KC - 1))
                    yield
                    if dst_f32 is not None:
                        copy('act', dst_f32[:, 0, tq * 512:(tq + 1) * 512], pb[:, :])
                    else:
                        act(sgt[:, tq * 512:(tq + 1) * 512], pb[:, :], AF.Sigmoid)
                    yield

            def prologue(h):
                sl = h % 2
                for part in range(3):
                    S.dma('sp', Rq[sl][part:part + 1, :], fsc_d[part, h:h + 1, :], reads=[dreg('fsc', part, part + 1)])
                    S.dma('sp', Lk[sl][3 + part:4 + part, :], fsc_d[3 + part, h:h + 1, :], reads=[dreg('fsc', 3 + part, 4 + part)])
                for wi_, c0_ in enumerate((h * 128, 1024 + h * 128, 3072 + h * 128)):
                    wload(wq[wi_], W[:, c0_:c0_ + 128])
                yield from proj_fm(wq[0], qf, None)
                yield from fm_rmsnorm_gen(qf, cols[:, C_FOXQ:C_FOXQ + 1], qn[sl], 1, T, HD, tmp,
                                          post_ln_bias=float(np.log(HD ** -0.5)), pbank=pro_banks)
                yield from proj_fm(wq[1], qf, None)
                yield from fm_rmsnorm_gen(qf, cols[:, C_FOXK:C_FOXK + 1], kn[sl], 1, T, HD, tmp, pbank=pro_banks)
                yield from proj_fm(wq[2], None, sg[sl])

            def attention(h):
                sl = h % 2
                tiles = [(qc, kb) for qc in range(4) for kb in range(4 * qc + 4)]
                nt = len(tiles)

                def info(i):
                    qc, kb = tiles[i]
                    q0 = qc * 512
                    par = (h * 4 + qc) % 2
                    n0 = max(0, kb - 4 * qc) * 128
                    return qc, kb, q0, psb[par], psb[2 + par], n0, kb >= 4 * qc, 4 * qc + 4

                def stA(i):
                    qc, kb, q0, po, pl, n0, diag, nkb = info(i)
                    pss = psb[4 + i % 2]
                    mm(pss[:, n0:512], kn[sl][:, 0, kb * 128:(kb + 1) * 128], qn[sl][:, 0, q0 + n0:q0 + 512], start=True, stop=False)
                    mm(pss[:, n0:512], Lk[sl][:, kb * 128:(kb + 1) * 128], Rq[sl][:, q0 + n0:q0 + 512], start=False, stop=not diag)
                    if diag:
                        mm(pss[:, n0:n0 + 128], ident_bf, causal_bf, start=False, stop=True)

                def stB(i):
                    qc, kb, q0, po, pl, n0, diag, nkb = info(i)
                    act(pT[i % 4][:, n0:512], psb[4 + i % 2][:, n0:512], AF.Exp)

                def stC(i):
                    qc, kb, q0, po, pl, n0, diag, nkb = info(i)
                    p = pT[i % 4]
                    mm(po[:, n0:512], Vtm[:, kb, h * 128:(h + 1) * 128], p[:, n0:512], start=(kb == 0), stop=(kb == nkb - 1))
                    mm(pl[:, n0:512], ones_bf, p[:, n0:512], start=(kb == 0), stop=(kb == nkb - 1))
                    if kb == nkb - 1:
                        recip(rc, pl[:, :])
                        tt('dve', ot, po[:, :], rc, ALU.mult)
                        tt('dve', mixT[:, h, q0:q0 + 512], ot, sg[sl][:, q0:q0 + 512], ALU.mult)

                for i in range(nt + 2):
                    if i < nt:
                        stA(i)
                    if 0 <= i - 2 < nt:
                        stC(i - 2)
                    if 0 <= i - 1 < nt:
                        stB(i - 1)
                    yield

            for _ in prologue(0):
                pass
            for h in range(8):
                gens = [attention(h)]
                if h + 1 < 8:
                    gens.append(prologue(h + 1))
                while gens:
                    for g_ in list(gens):
                        try:
                            next(g_)
                        except StopIteration:
                            gens.remove(g_)

        def phase_dn():
            W = dnw_d
            ps_lim[0] = 7
            pcs = psb[7]
            o = OFF_WORK
            colq = view(o, [128, 4, 16, 8], F32); o += 2048
            egl = view(o, [128, 8, 32], F32); o += 1024
            cf = view(o, [128, 5, 2, 16], F32); o += 640
            wab = view(o, [128, KC, 16], BF16); o += 256
            nea = view(o, [128, 16], F32)[0:8]; o += 64
            id2 = view(o, [128, 2, 128], F32); o += 1024
            msk = [view(o + i * 512, [128, 2, 128], BF16) for i in range(3)]; o += 1536
            sc = view(o, [128, 64], F32); o += 256
            RR = [view(o + i * 4096, [128, T], BF16) for i in range(3)]; o += 12288
            for i in range(3):
                memset('dve', RR[i], 0.0)
            o_grp = o
            ident_f = cst[:, K_IDENT:K_IDENT + 128]
            for hl in range(2):
                copy('dve', id2[:, hl, :], ident_f)
                for i, kk_ in enumerate((K_MLOW, K_MUPS, K_MUPI)):
                    copy('dve', msk[i][:, hl, :], cst[:, kk_:kk_ + 128])
            bd6 = cbf[:, K_BD6:K_BD6 + 256]
            on6 = cbf[:, K_ON6:K_ON6 + 128]
            e6 = cst[0:6, K_E6:K_E6 + 2]
            selg = cbf[:, K_SELG:K_SELG + 4].rearrange("p (a b) -> p a b", a=2)
            onesrow = cst[0:8, K_ONES:K_ONES + 64]

            rA, rB, rC, rD, rE, rF = [view(o_grp + i * 8192, [128, T], F32)[0:8] for i in range(6)]
            wload(wab, W[:, 4096:4112])
            for tq in range(4):
                pb = bank()
                for kc in range(KC):
                    mm(pb[0:8, :], wab[:, kc, 0:8], hT[:, kc, tq * 512:(tq + 1) * 512], start=(kc == 0), stop=(kc == KC - 1))
                act(rA[:, tq * 512:(tq + 1) * 512], pb[0:8, :], AF.Identity, bias=cols[0:8, C_HS + 1:C_HS + 2])
                pb = bank()
                for kc in range(KC):
                    mm(pb[0:8, :], wab[:, kc, 8:16], hT[:, kc, tq * 512:(tq + 1) * 512], start=(kc == 0), stop=(kc == KC - 1))
                copy('act', rB[:, tq * 512:(tq + 1) * 512], pb[0:8, :])
            stt(rC, rA, -1.0, rA, ALU.mult, ALU.min)
            act(rC, rC, AF.Exp)
            act(rC, rC, AF.Ln, bias=1.0)
            stt(rA, rA, 0.0, rC, ALU.max, ALU.add)
            act(nea[:, 0:1], cols[0:8, C_HS:C_HS + 1], AF.Exp)
            ts('dve', nea[:, 0:1], nea[:, 0:1], -1.0, ALU.mult)
            ts('dve', rA, rA, nea[:, 0:1], ALU.mult)
            for c in range(32):
                scan(rD[:, c * 64:(c + 1) * 64], onesrow, rA[:, c * 64:(c + 1) * 64], 0.0, ALU.mult, ALU.add)
            stt(rC, rB, -1.0, rB, ALU.mult, ALU.min)
            act(rC, rC, AF.Exp)
            act(rC, rC, AF.Ln, bias=1.0)
            stt(rB, rB, 0.0, rC, ALU.min, ALU.subtract)
            tt('dve', rC, rD, rB, ALU.add)
            S.dma('sp', rows_d[0], rC, writes=[dreg('rowsd', 0, 1)])
            ts('dve', rE, rD, -1.0, ALU.mult)
            S.dma('sp', rows_d[1], rE, writes=[dreg('rowsd', 1, 2)])
            S.dma('sp', rows_d[2], rD, writes=[dreg('rowsd', 2, 3)])
            act(rB, rB, AF.Exp)
            act(rE, rD, AF.Exp)
            tt('dve', rC, rB, rE, ALU.mult)
            for c in range(32):
                ts('dve', rF[:, c * 64:(c + 1) * 64], rD[:, c * 64:(c + 1) * 64], -1.0, ALU.mult,
                   rD[:, c * 64 + 63:c * 64 + 64], ALU.add)
            act(rF, rF, AF.Exp)
            pcol = bank()
            for q, row in enumerate((rC, rB, rF, rE)):
                for tl in range(16):
                    cidx = (q * 16 + tl) * 8
                    mm(pcol[:, cidx:cidx + 8], row[0:8, tl * 128:(tl + 1) * 128], cst[0:8, K_IDENT:K_IDENT + 8])
            copy('dve', colq.rearrange("p a b c -> p (a b c)"), pcol[:, :])
            pe_ = bank()
            gl = rD.rearrange("p (c k) -> p c k", k=64)[:, :, 63]
            for h in range(8):
                mm(pe_[:, h * 32:(h + 1) * 32], cst[0:8, K_SEL8 + h * 128:K_SEL8 + (h + 1) * 128], gl)
            act(egl.rearrange("p a b -> p (a b)"), pe_[:, 0:256], AF.Exp)

            if dbg == 'dn_a':
                ps_lim[0] = 8
                return
            for g in range(4):
                if dbg is not None and dbg[:4] in ('dn_b', 'dn_c', 'dn_d', 'dn_e') and g > 0:
                    break
                o = o_grp
                QT = view(o, [128, 2, T], BF16); o += 8192
                KT = view(o, [128, 2, T], BF16); o += 8192
                VT = view(o, [128, 2, T], BF16); o += 8192
                R1 = view(o, [128, T], F32)[0:2]; o += 8192
                R2b = view(o, [128, T], F32)[0:2]; o += 8192
                R3 = view(o, [128, T], F32)[0:2]
                ssk = view(o, [128, T], F32)[0:2]; o += 8192
                o_conv = o
                Ppre2 = [view(o + i * 4112, [128, 2056], BF16) for i in range(2)]; o += 8224
                dg = [view(o + i * 1024, [128, 4, 128], BF16) for i in range(2)]; o += 2048
                wbf = [view(o + i * 2048, [128, KC, 128], BF16) for i in range(2)]; o += 4096
                sqt = [view(o + i * 1024, [128, 512], BF16) for i in range(2)]; o += 2048
                sgt = sqt
                o = o_conv
                exE = [[view(o + (j * 3 + i) * 1024, [128, 2, 128], F32) for i in range(3)] for j in range(2)]; o += 6144
                def nview(e0, shape, dt):
                    n_ = 1
                    for s_ in shape[1:]:
                        n_ *= s_
                    ap = nmn[:, e0:e0 + n_]
                    if dt != F32R:
                        ap = ap.bitcast(dt)
                    if len(shape) == 3:
                        return ap.rearrange("p (a b) -> p a b", a=shape[1])
                    return ap.rearrange("p (a b c) -> p a b c", a=shape[1], b=shape[2])
                Am = [[nview((j * 2 + i) * 256, [128, 2, 128], F32) for i in range(2)] for j in range(2)]
                AmR = [[nview((j * 2 + i) * 256, [128, 2, 128], F32R) for i in range(2)] for j in range(2)]
                BX = [[nview(1024 + (j * 2 + i) * 512, [128, 2, 2, 128], F32) for i in range(2)] for j in range(2)]
                BXR = [[nview(1024 + (j * 2 + i) * 512, [128, 2, 2, 128], F32R) for i in range(2)] for j in range(2)]
                BDt = [view(o + j * 1536, [128, 3, 2, 128], BF16) for j in range(2)]; o += 3072
                o_pc = o
                kbg = [view(o + i * 512, [128, 2, 128], BF16) for i in range(3)]; o += 1536
                kdec = [view(o + i * 512, [128, 2, 128], BF16) for i in range(3)]; o += 1536
                vb = [view(o + i * 512, [128, 2, 128], BF16) for i in range(3)]; o += 1536
                QKm = [view(o + i * 512, [128, 2, 128], BF16) for i in range(3)]; o += 1536
                TT = [view(o + i * 512, [128, 2, 128], BF16) for i in range(3)]; o += 1536
                wT = [view(o + i * 512, [128, 2, 128], BF16) for i in range(3)]; o += 1536
                u = [view(o + i * 1024, [128, 2, 128], F32) for i in range(3)]; o += 3072
                Sst = view(o, [128, 2, 128], F32); o += 1024
                Sbf = view(o, [128, 2, 128], BF16); o += 512
                vnew = view(o, [128, 2, 128], BF16); o += 512
                otm = view(o, [128, 2, 128], F32); o += 1024
                otmp = view(o, [128, 2, 128], F32); o += 1024
                osq = view(o, [128, 2, 128], F32); o += 1024
                onb = view(o, [128, 2, 128], BF16); o += 512
                ss = view(o, [128, 16], F32); o += 64

                for i in range(2):
                    memset('dve', Ppre2[i][:, 0:4], 0.0)
                memset('dve', ssk, 0.0)
                memset('dve', Sst, 0.0)
                memset('dve', Sbf, 0.0)
                wcnt = 0
                def row_section():
                    copy('dve', sc, pcs[:, 0:64])
                    act(sc, sc, AF.Ln, bias=EPS)
                    act(sc[:, 0:32], sc[:, 0:32], AF.Exp, scale=-0.5, bias=float(np.log(HD ** -0.5)))
                    act(sc[:, 32:64], sc[:, 32:64], AF.Exp, scale=-0.5)
                    rq = sc[:, 0:32].rearrange("p (a b) -> p a b", a=2)
                    rk = sc[:, 32:64].rearrange("p (a b) -> p a b", a=2)
                    for hl in range(2):
                        h = 2 * g + hl
                        tt('dve', cf[:, 0, hl, :], rk[:, hl, :], colq[:, 0, :, h], ALU.mult)
                        tt('dve', cf[:, 1, hl, :], rk[:, hl, :], colq[:, 2, :, h], ALU.mult)
                        tt('dve', cf[:, 2, hl, :], rq[:, hl, :], colq[:, 3, :, h], ALU.mult)
                        copy('dve', cf[:, 3, hl, :], rq[:, hl, :])
                        copy('dve', cf[:, 4, hl, :], colq[:, 1, :, h])
                    act(ssk, ssk, AF.Ln, bias=EPS)
                    ts('dve', ssk, ssk, -0.5, ALU.mult)
                    S.dma('sp', R1, rows_d[0, 2 * g:2 * g + 2, :], reads=[dreg('rowsd', 0, 1)])
                    S.dma('sp', R2b, rows_d[1, 2 * g:2 * g + 2, :], reads=[dreg('rowsd', 1, 2)])
                    tt('dve', R1, R1, ssk, ALU.add)
                    tt('dve', R2b, R2b, ssk, ALU.add)
                    S.dma('sp', R3, rows_d[2, 2 * g:2 * g + 2, :], reads=[dreg('rowsd', 2, 3)])
                    pcb = [view(o_conv + 16416, [128, T], BF16)[0:2], view(ARENA_BYTES - 4096, [128, T], BF16)[0:2]]
                    for ri, Rr in enumerate((R1, R2b, R3)):
                        for p in range(3):
                            pc = pcb[(ri * 3 + p) % 2]
                            copy('dve', pc, Rr)
                            S.dma('sp', RR[ri][2 * p:2 * p + 2, :], pc)
                            if p < 2:
                                tt('dve', Rr, Rr, pc, ALU.subtract)

                for kind, base, dst in ((0, 0, QT), (1, 1024, KT), (2, 2048, VT)):
                    if kind == 2:
                        row_section()
                    for hl in range(2):
                        h = 2 * g + hl
                        ch = kind * 8 + h
                        wbb = wbf[wcnt % 2]
                        dgt = dg[wcnt % 2]
                        Ppre = Ppre2[wcnt % 2]
                        wcnt += 1
                        wload(wbb, W[:, base + h * 128:base + (h + 1) * 128])
                        for tq in range(4):
                            pb = bank()
                            for kc in range(KC):
                                mm(pb[:, :], wbb[:, kc, :], hT[:, kc, tq * 512:(tq + 1) * 512], start=(kc == 0), stop=(kc == KC - 1))
                            copy('act' if (tq % 2 or kind == 2) else 'dve', Ppre[:, 4 + tq * 512:4 + (tq + 1) * 512], pb[:, :])
                        for j in range(4):
                            ts('pool', dgt[:, j, :], ident_bf, cols[:, C_CONV + j * 24 + ch:C_CONV + j * 24 + ch + 1], ALU.mult, 1.0, ALU.mult)
                        pending = None
                        for tq in range(4):
                            pb = bank()
                            for j in range(4):
                                mm(pb[:, :], dgt[:, j, :], Ppre[:, tq * 512 + 1 + j:tq * 512 + 1 + j + 512], start=(j == 0), stop=(j == 3))
                            dsl = dst[:, hl, tq * 512:(tq + 1) * 512]
                            act(dsl, pb[:, :], AF.Silu)
                            if pending is not None:
                                pending()
                                pending = None
                            if kind < 2:
                                sq = sqt[tq % 2]
                                tt('dve', sq, dsl, dsl, ALU.mult)

                                def pending(sq=sq, tq=tq):
                                    for t4 in range(4):
                                        col = (kind * 2 + hl) * 16 + tq * 4 + t4
                                        mm(pcs[:, col:col + 1], sq[:, t4 * 128:(t4 + 1) * 128], ones_bf[:, 0:1])
                                    if kind == 1:
                                        pr = bank()
                                        mm(pr[0:2, :], selg[:, hl, :], sq)
                                        tt('dve', ssk[:, tq * 512:(tq + 1) * 512], pr[0:2, :], ssk[:, tq * 512:(tq + 1) * 512], ALU.add)
                        if pending is not None:
                            pending()
                for j in range(2):
                    memset('dve', BDt[j], 0.0)

                def v3(ap):
                    return ap.rearrange("p (a b) -> p a b", a=2)

                def prep(n):
                    bs = n % 3
                    t2 = n % 2
                    exE_, Am_, BX_, BDt_ = exE[t2], Am[t2], BX[t2], BDt[t2]
                    AmR_, BXR_ = AmR[t2], BXR[t2]
                    b0, b1, b2 = psb[3 * t2], psb[3 * t2 + 1], psb[3 * t2 + 2]
                    tsl = slice(n * 128, (n + 1) * 128)
                    for i, ri in enumerate((1, 0, 2)):
                        for hl in range(2):
                            ts('pool', BDt_[0:6, i, hl, :], RR[ri][0:6, tsl], e6[:, hl:hl + 1], ALU.mult, 1.0, ALU.mult)
                    yield
                    ptb = b2[:].bitcast(BF16)[:, 512:1024]
                    for hl in range(2):
                        transpose(ptb[:, hl * 128:(hl + 1) * 128], KT[:, hl, tsl], ident_bf)
                        transpose(ptb[:, 256 + hl * 128:256 + (hl + 1) * 128], VT[:, hl, tsl], ident_bf)
                    pkk = b0[:, 0:256]
                    pqk = b0[:, 256:512]
                    for hl in range(2):
                        mm(pkk[:, hl * 128:(hl + 1) * 128], KT[:, hl, tsl], KT[:, hl, tsl])
                    for hl in range(2):
                        mm(pqk[:, hl * 128:(hl + 1) * 128], KT[:, hl, tsl], QT[:, hl, tsl])
                    specs = ((RR[0], 0, 0, b1[:, 0:256]), (RR[1], 1, 1, b1[:, 256:512]), (RR[1], 2, 2, b2[:, 0:256]))
                    pes = []
                    for lrow, bdi, mi, pe in specs:
                        mm(pe, lrow[:, tsl], bd6, start=True, stop=False)
                        mm(pe, on6, BDt_[:, bdi].rearrange("p a b -> p (a b)"), start=False, stop=False)
                        mm(pe, ident_bf, msk[mi].rearrange("p a b -> p (a b)"), start=False, stop=True)
                        pes.append(pe)
                    yield
                    for hl in range(2):
                        act(kbg[bs][:, hl, :], ptb[:, hl * 128:(hl + 1) * 128], AF.Copy, scale=cf[:, 0, hl, n:n + 1])
                        act(kdec[bs][:, hl, :], ptb[:, hl * 128:(hl + 1) * 128], AF.Copy, scale=cf[:, 1, hl, n:n + 1])
                        act(vb[bs][:, hl, :], ptb[:, 256 + hl * 128:256 + (hl + 1) * 128], AF.Copy, scale=cf[:, 4, hl, n:n + 1])
                    act(exE_[0], v3(pes[0]), AF.Exp)
                    act(exE_[1], v3(pes[1]), AF.Exp)
                    act(exE_[2], v3(pes[2]), AF.Exp)
                    yield
                    tt('dve', AmR_[0], v3(pkk), exE_[0], ALU.mult)
                    tt('dve', BXR_[0][:, :, 0, :], v3(pkk), exE_[1], ALU.mult)
                    stt(BXR_[1][:, :, 1, :], BX_[0][:, :, 0, :], -1.0, id2, ALU.mult, ALU.add)
                    tt('dve', QKm[bs], v3(pqk), exE_[2], ALU.mult)
                    yield
                    pa = b0
                    pbb = b1
                    for hl in range(2):
                        mm(pa[:, hl * 128:(hl + 1) * 128], BXR_[0][:, hl, 0, :], AmR_[0][:, hl, :])
                        mm(pbb[:, hl * 128:(hl + 1) * 128], AmR_[0][:, hl, :], BXR_[0][:, hl, 0, :])
                    yield
                    copy('act', AmR_[1], v3(pa[:, 0:256]))
                    copy('act', BXR_[1][:, :, 0, :], v3(pbb[:, 0:256]))
                    yield
                    cur = 1
                    for st in range(1, 6):
                        nx = 1 - cur
                        if st <= 3:
                            pbx = b2 if st % 2 else b1
                            pa = b0
                            for hl in range(2):
                                mm(pbx[:, hl * 256:(hl + 1) * 256], AmR_[cur][:, hl, :], BXR_[cur][:, hl, :, :].rearrange("p a b -> p (a b)"))
                                mm(pa[:, hl * 128:(hl + 1) * 128], BXR_[cur][:, hl, 0, :], AmR_[cur][:, hl, :])
                            yield
                            p4 = pbx[:, :].rearrange("p (h t c) -> p h t c", h=2, t=2)
                            copy('act', AmR_[nx], v3(pa[:, 0:256]))
                            tt('dve', BXR_[nx][:, :, 1, :], p4[:, :, 1, :], BX_[cur][:, :, 1, :], ALU.add)
                            copy('act', BXR_[nx][:, :, 0, :], p4[:, :, 0, :])
                            yield
                        elif st == 4:
                            pbx = b1
                            pa = b0
                            for hl in range(2):
                                mm(pbx[:, hl * 128:(hl + 1) * 128], AmR_[cur][:, hl, :], BXR_[cur][:, hl, 1, :])
                                mm(pa[:, hl * 128:(hl + 1) * 128], BXR_[cur][:, hl, 0, :], AmR_[cur][:, hl, :])
                            yield
                            copy('act', AmR_[nx], v3(pa[:, 0:256]))
                            tt('dve', BXR_[nx][:, :, 1, :], v3(pbx[:, 0:256]), BX_[cur][:, :, 1, :], ALU.add)
                            yield
                        else:
                            pbx = b2
                            for hl in range(2):
                                mm(pbx[:, hl * 128:(hl + 1) * 128], AmR_[cur][:, hl, :], BXR_[cur][:, hl, 1, :])
                            yield
                            tt('dve', TT[bs], v3(pbx[:, 0:256]), BX_[cur][:, :, 1, :], ALU.add)
                            yield
                        cur = nx
                    pu = b0
                    pw = b1
                    for hl in range(2):
                        mm(pu[:, hl * 128:(hl + 1) * 128], TT[bs][:, hl, :], vb[bs][:, hl, :])
                        mm(pw[:, hl * 128:(hl + 1) * 128], kbg[bs][:, hl, :], TT[bs][:, hl, :])
                    yield
                    copy('act', u[bs], v3(pu[:, 0:256]))
                    copy('dve', wT[bs], v3(pw[:, 0:256]))
                    yield

                def make_post(n):
                    pt2b = psb[6][:].bitcast(BF16)[:, 512:1024]

                    def p1():
                        tt('pool', osq, otm, otm, ALU.mult)
                        S.op('dve', lambda: nc.vector.tensor_reduce(out=ss[:, 0:2], in_=osq, axis=AX.X, op=ALU.add),
                             reads=[osq], writes=[ss[:, 0:2]])

                    def p2():
                        act(ss[:, 0:2], ss[:, 0:2], AF.Ln, scale=1.0 / HD, bias=EPS)
                        act(ss[:, 0:2], ss[:, 0:2], AF.Exp, scale=-0.5)

                    def p3():
                        for hl in range(2):
                            ts('pool', onb[:, hl, :], otm[:, hl, :], ss[:, hl:hl + 1], ALU.mult, 1.0, ALU.mult)

                    def p4():
                        for hl in range(2):
                            transpose(pt2b[:, hl * 128:(hl + 1) * 128], onb[:, hl, :], ident_bf)

                    def p5():
                        for hl in range(2):
                            act(mixT[:, 2 * g + hl, n * 128:(n + 1) * 128], pt2b[:, hl * 128:(hl + 1) * 128], AF.Copy,
                                scale=cols[:, C_DNO:C_DNO + 1])
                    return [p1, p2, p3, p4, p5]

                def scan_tile(n, post):
                    bs = n % 3
                    pend = None
                    for half in range(2):
                        cb = half * 64
                        c = 2 * n + half
                        ps_ = slice(cb, cb + 64)
                        pv = psb[6]
                        po = psb[7]
                        for hl in range(2):
                            mm(pv[ps_, hl * 128:(hl + 1) * 128], wT[bs][:, hl, cb:cb + 64], Sbf[:, hl, :])
                        if half == 0 and post is not None:
                            post[0]()
                        if pend is not None:
                            pend()
                            pend = None
                        for hl in range(2):
                            mm(po[ps_, hl * 256:hl * 256 + 128], QT[:, hl, n * 128 + cb:n * 128 + cb + 64], Sbf[:, hl, :])
                        if half == 1 and post is not None:
                            post[3]()
                        yield
                        tt('dve', vnew[ps_, :, :], u[bs][ps_, :, :], v3(pv[ps_, 0:256]), ALU.subtract)
                        if post is not None:
                            post[1 if half == 0 else 4]()
                        yield
                        pS = psb[6][:, 256:512]
                        for hl in range(2):
                            mm(pS[:, hl * 128:(hl + 1) * 128], kdec[bs][ps_, hl, :], vnew[ps_, hl, :])
                        for hl in range(2):
                            mm(po[ps_, hl * 256 + 128:hl * 256 + 256], QKm[bs][ps_, hl, cb:cb + 64], vnew[ps_, hl, :])
                        if half == 0 and post is not None:
                            post[2]()
                        yield
                        for hl in range(2):
                            h = 2 * g + hl
                            stt(Sbf[:, hl, :], Sst[:, hl, :], egl[:, h, c:c + 1], pS[:, hl * 128:(hl + 1) * 128], ALU.mult, ALU.add)
                        for hl in range(2):
                            act(otmp[ps_, hl, :], po[ps_, hl * 256:hl * 256 + 128], AF.Copy, scale=cf[ps_, 2, hl, n:n + 1])
                        for hl in range(2):
                            h = 2 * g + hl
                            stt(Sst[:, hl, :], Sst[:, hl, :], egl[:, h, c:c + 1], pS[:, hl * 128:(hl + 1) * 128], ALU.mult, ALU.add)
                        yield

                        def pend(ps_=ps_, po=po):
                            for hl in range(2):
                                stt(otm[ps_, hl, :], po[ps_, hl * 256 + 128:hl * 256 + 256], cf[ps_, 3, hl, n:n + 1], otmp[ps_, hl, :], ALU.mult, ALU.add)
                    pend()

                def drain(gen):
                    for _ in gen:
                        pass

                live = {}
                for n0 in (0, 1):
                    live[n0] = prep(n0)
                drain(live.pop(0))
                for n in range(16):
                    if n + 2 < 16:
                        live[n + 2] = prep(n + 2)
                    sc_ = scan_tile(n, make_post(n - 1) if n > 0 else None)
                    while True:
                        try:
                            next(sc_)
                        except StopIteration:
                            break
                        for k_ in sorted(live):
                            try:
                                next(live[k_])
                            except StopIteration:
                                del live[k_]
                    if n + 1 in live:
                        drain(live.pop(n + 1))
                for p_ in make_post(15):
                    p_()

                for hl in range(2):
                    h = 2 * g + hl
                    wbb = wbf[hl]
                    wload(wbb, W[:, 3072 + h * 128:3072 + (h + 1) * 128])
                    for tq in range(4):
                        pb = bank()
                        for kc in range(KC):
                            mm(pb[:, :], wbb[:, kc, :], hT[:, kc, tq * 512:(tq + 1) * 512], start=(kc == 0), stop=(kc == KC - 1))
                        sg_ = sgt[tq % 2]
                        act(sg_, pb[:, :], AF.Silu)
                        tt('dve', mixT[:, h, tq * 512:(tq + 1) * 512], mixT[:, h, tq * 512:(tq + 1) * 512], sg_, ALU.mult)
            ps_lim[0] = 8

        def phase_dn_stub():
            for h in range(8):
                memset('dve', mixT[:, h, :], 0.0)

        phase_mem()
        for li, layer in enumerate(layers):
            if li == 0:
                x_src, x_name = xT_d, None
            else:
                x_src, x_name = xs2_d, 'xs2'
            phase_norm1(layer, x_src, x_name)
            if layer == 0:
                if dbg == 'nodn':
                    phase_dn_stub()
                else:
                    phase_dn()
                if dbg is not None and dbg.startswith('dn'):
                    for k in range(8):
                        S.dma('sp', dbg_d[k * 128:(k + 1) * 128, :], mixT[:, k, :], is_output=True)
                    break
                phase_memattn(layer, dnw_d, 4 * 1024 + 16, OFF_WORK)
            else:
                phase_fox()
                phase_memattn(layer, foxw_d, 4 * 1024 + 8, OFF_WORK)
            if dbg == f'mix{layer}':
                for k in range(12):
                    S.dma('sp', dbg_d[k * 128:(k + 1) * 128, :], mixT[:, k, :], is_output=True)
            phase_outproj(layer, x_src, x_name)
            last = (li == len(layers) - 1)
            if last:
                phase_mlp(layer, outT_d, None, True)
            else:
                phase_mlp(layer, xs2_d, 'xs2', False)
        S.finish()
        print("instr counts", S.n_inst, "waits", S.n_wait)
    return nc


def host_pack(inputs):
    f = np.float32
    cols = np.zeros((128, NCOLS), f)

    def chunked(v):
        return np.ascontiguousarray(np.asarray(v, f).reshape(-1, 128).T)

    for l in range(2):
        cols[:, C_N1 + l * 8:C_N1 + l * 8 + 8] = chunked(inputs['norm1_w'][l])
        cols[:, C_N2 + l * 8:C_N2 + l * 8 + 8] = chunked(inputs['norm2_w'][l])
        cols[:, C_MEMQ + l] = inputs['memq_norm_w'][l]
    cols[:, C_MEMN:C_MEMN + 8] = chunked(inputs['mem_norm_w'])
    cols[:, C_MEMK] = inputs['mem_k_norm_w']
    cols[:, C_FOXQ] = inputs['fox_q_norm_w'][0]
    cols[:, C_FOXK] = inputs['fox_k_norm_w'][0]
    cols[:, C_DNO] = inputs['dn_o_norm_w'][0]
    for j in range(4):
        cols[:, C_CONV + j * 24:C_CONV + (j + 1) * 24] = chunked(inputs['dn_conv_w'][0][j])
    cols[0:8, C_HS + 0] = inputs['dn_a_log'][0]
    cols[0:8, C_HS + 1] = inputs['dn_dt_bias'][0]
    cols[0:8, C_HS + 2] = inputs['fox_f_bias'][0]

    consts = np.zeros((128, NCONST), f)
    consts[:, K_IDENT:K_IDENT + 128] = np.eye(128, dtype=f)
    kk, qq = np.meshgrid(np.arange(128), np.arange(128), indexing='ij')
    consts[:, K_CAUSAL:K_CAUSAL + 128] = np.where(kk <= qq, 0.0, NEG)
    consts[:, K_ONES:K_ONES + 128] = 1.0
    same = (kk // 64) == (qq // 64)
    consts[:, K_MLOW:K_MLOW + 128] = np.where(same & (kk > qq), 0.0, NEG)
    consts[:, K_MUPS:K_MUPS + 128] = np.where(same & (qq > kk), 0.0, NEG)
    consts[:, K_MUPI:K_MUPI + 128] = np.where(same & (qq >= kk), 0.0, NEG)
    sel8 = np.zeros((8, 8, 128), f)
    for h in range(8):
        sel8[h, h, :] = 1.0
    consts[0:8, K_SEL8:K_SEL8 + 1024] = sel8.reshape(8, 1024)
    selg = np.zeros((128, 2, 2), f)
    selg[:, 0, 0] = 1.0
    selg[:, 1, 1] = 1.0
    consts[:, K_SELG:K_SELG + 4] = selg.reshape(128, 4)
    bd6 = np.zeros((128, 2, 128), f)
    for p in range(3):
        for r in range(2):
            bd6[2 * p + r, r, :] = 1.0
            consts[2 * p + r, K_E6 + r] = 1.0
    consts[:, K_BD6:K_BD6 + 256] = bd6.reshape(128, 256)
    consts[0:6, K_ON6:K_ON6 + 128] = 1.0
    return cols, consts


_NC_CACHE = {}


def kernel(**inputs):
    inputs = {k: np.asarray(v) for k, v in inputs.items()}
    cols, consts = host_pack(inputs)
    if 'nc' not in _NC_CACHE:
        _NC_CACHE['nc'] = build_program()
    nc = _NC_CACHE['nc']
    shared = dict(
        cols=cols, consts=consts,
        w_mem_kv=np.ascontiguousarray(inputs['w_mem_kv'], dtype=np.float32),
        dn_w_in=np.ascontiguousarray(inputs['dn_w_in'][0], dtype=np.float32),
        fox_w_in=np.ascontiguousarray(inputs['fox_w_in'][0], dtype=np.float32),
        w_out=np.ascontiguousarray(inputs['w_out'], dtype=np.float32),
        w_mlp1=np.ascontiguousarray(inputs['w_mlp1'], dtype=np.float32),
        w_mlp2=np.ascontiguousarray(inputs['w_mlp2'], dtype=np.float32),
    )
    in_maps = []
    for b in range(N_CORES):
        m = dict(shared)
        m['xT'] = np.ascontiguousarray(inputs['x'][b].T, dtype=np.float32)
        m['memT'] = np.ascontiguousarray(inputs['mem'][b].T, dtype=np.float32)
        in_maps.append(m)
    res = run_bass_kernel_spmd(nc, in_maps, core_ids=list(range(N_CORES)))
    out = np.stack([np.ascontiguousarray(res.results[b]['outT'].T) for b in range(N_CORES)], axis=0)
    return out.astype(np.float32)
```
